# Optimizing a Trainium2 kernel written in Bass

```python
import math
import jax, jax.numpy as jnp
from jax import lax
import numpy as np

D_MODEL = 2048
BATCH = 4
SEQ = 2048
DEPTH = 1
DEC_BATCH = 32
DEC_SEQ = 1
PAST_LEN = 16384
PAGE_SIZE = 128

HEAD_DIM = 64
N_HEADS = D_MODEL // 128
N_KV_HEADS = 4
GQA_GROUP = N_HEADS // N_KV_HEADS
WINDOW = 128
ATTN_BLOCK = WINDOW
ATTN_W = N_HEADS * HEAD_DIM
KV_W = N_KV_HEADS * HEAD_DIM
CHUNK = 128
GM_GROUPS = 4
GM_W = D_MODEL // 2
GM_GROUP_W = GM_W // GM_GROUPS
D_FF = (11 * D_MODEL) // 4
CONV_W = 3
PLE_DIM = 256
IN_W = ATTN_W + 2 * KV_W + 2 * GM_W
SPLITS = (ATTN_W, ATTN_W + KV_W, ATTN_W + 2 * KV_W, ATTN_W + 2 * KV_W + GM_W)
EPS = 1e-6
MASK_VALUE = -1e30

kernel_name = "hybrid_swa_sgu_convffn_decode_step"


def rmsnorm(x, w):
    xf = x.astype(jnp.float32)
    y = xf * lax.rsqrt(jnp.mean(xf * xf, axis=-1, keepdims=True) + EPS)
    return (y * w.astype(jnp.float32)).astype(x.dtype)


def layernorm(x, w):
    xf = x.astype(jnp.float32)
    mu = jnp.mean(xf, axis=-1, keepdims=True)
    xc = xf - mu
    y = xc * lax.rsqrt(jnp.mean(xc * xc, axis=-1, keepdims=True) + EPS)
    return (y * w.astype(jnp.float32)).astype(x.dtype)


def alibi_slopes():
    h = jnp.arange(1, N_HEADS + 1, dtype=jnp.float32)
    return jnp.exp2(-8.0 * h / N_HEADS)


def sink_attention(q, k, v, dist, valid, sinks):
    s = jnp.einsum('...ikgd,...jkd->...kgij', q, k).astype(jnp.float32) * (HEAD_DIM ** -0.5)
    slopes = alibi_slopes().reshape(N_KV_HEADS, GQA_GROUP, 1, 1)
    s = s - slopes * dist.astype(jnp.float32)
    s = jnp.where(valid, s, MASK_VALUE)
    sink_col = jnp.broadcast_to(sinks.astype(jnp.float32).reshape(N_KV_HEADS, GQA_GROUP, 1, 1), s.shape[:-1] + (1,))
    probs = jax.nn.softmax(jnp.concatenate([s, sink_col], axis=-1), axis=-1)[..., :-1]
    return jnp.einsum('...kgij,...jkd->...ikgd', probs.astype(v.dtype), v)


def attn_prompt(q, k, v, sinks):
    B, L = q.shape[:2]
    nb = L // ATTN_BLOCK
    qb = q.reshape(B, nb, ATTN_BLOCK, N_KV_HEADS, GQA_GROUP, HEAD_DIM)
    pad = jnp.zeros((B, ATTN_BLOCK, N_KV_HEADS, HEAD_DIM), k.dtype)
    kb = jnp.concatenate([pad, k], axis=1).reshape(B, nb + 1, ATTN_BLOCK, N_KV_HEADS, HEAD_DIM)
    vb = jnp.concatenate([pad, v], axis=1).reshape(B, nb + 1, ATTN_BLOCK, N_KV_HEADS, HEAD_DIM)
    kk = jnp.concatenate([kb[:, :-1], kb[:, 1:]], axis=2)
    vv = jnp.concatenate([vb[:, :-1], vb[:, 1:]], axis=2)
    i = jnp.arange(ATTN_BLOCK)[:, None]
    j = jnp.arange(2 * ATTN_BLOCK)[None, :]
    dist = ATTN_BLOCK + i - j
    c = jnp.arange(nb)[:, None, None]
    valid = (dist >= 0) & (dist <= WINDOW) & (c * ATTN_BLOCK + j - ATTN_BLOCK >= 0)
    out = sink_attention(qb, kk, vv, dist, valid[:, None, None], sinks)
    return out.reshape(B, L, ATTN_W), k[:, -WINDOW:], v[:, -WINDOW:]


def attn_sample(q, k, v, k_buf, v_buf, sinks):
    B, T = q.shape[:2]
    kk = jnp.concatenate([k_buf, k], axis=1)
    vv = jnp.concatenate([v_buf, v], axis=1)
    i = jnp.arange(T)[:, None]
    j = jnp.arange(WINDOW + T)[None, :]
    dist = WINDOW + i - j
    valid = (dist >= 0) & (dist <= WINDOW)
    qg = q.reshape(B, T, N_KV_HEADS, GQA_GROUP, HEAD_DIM)
    out = sink_attention(qg, kk, vv, dist, valid, sinks)
    return out.reshape(B, T, ATTN_W), kk[:, T:], vv[:, T:]


def spatial_mix(vg, w_s, b_s):
    Lc = vg.shape[2]
    mask = jnp.tril(jnp.ones((Lc, Lc), dtype=bool))
    w = jnp.where(mask, w_s[:, :Lc, :Lc], 0).astype(vg.dtype)
    bias = jnp.transpose(b_s[:, :Lc])[:, :, None].astype(vg.dtype)
    return jnp.einsum('gts,bcsgd->bctgd', w, vg) + bias


def layer(x, p, k_buf, v_buf, conv_buf, lw, is_prompt):
    B, L, _ = x.shape
    xn = rmsnorm(x, lw['attn_norm_w'])
    proj = xn @ lw['w_in']
    q, k, v, gu, gv = jnp.split(proj, SPLITS, axis=-1)
    q = rmsnorm(q.reshape(B, L, N_HEADS, HEAD_DIM), lw['q_norm_w'])
    k = rmsnorm(k.reshape(B, L, N_KV_HEADS, HEAD_DIM), lw['k_norm_w'])
    v = v.reshape(B, L, N_KV_HEADS, HEAD_DIM)
    if is_prompt:
        attn, k_new, v_new = attn_prompt(q, k, v, lw['attn_sinks'])
        n_chunks, chunk_len = L // CHUNK, CHUNK
    else:
        attn, k_new, v_new = attn_sample(q, k, v, k_buf, v_buf, lw['attn_sinks'])
        n_chunks, chunk_len = 1, L
    gu = jax.nn.gelu(gu)
    gv = layernorm(jax.nn.gelu(gv), lw['sgu_norm_w'])
    vg = gv.reshape(B, n_chunks, chunk_len, GM_GROUPS, GM_GROUP_W)
    sgu = gu * spatial_mix(vg, lw['sgu_w'], lw['sgu_b']).reshape(B, L, GM_W)
    sgu_state = gv[:, -chunk_len:]
    branch_attn = attn @ lw['w_br_attn']
    branch_sgu = sgu @ lw['w_br_gm']
    gates = jax.nn.sigmoid(xn @ lw['w_gate'] + lw['b_gate'])
    merged = gates[..., D_MODEL:] * branch_attn + gates[..., :D_MODEL] * branch_sgu
    x = x + merged @ lw['w_out']
    h = rmsnorm(x, lw['ffn_norm_w']) @ lw['w_up']
    if conv_buf is None:
        conv_buf = jnp.zeros((B, CONV_W - 1, 2 * D_FF), h.dtype)
    hp = jnp.concatenate([conv_buf, h], axis=1)
    cw = lw['conv_w']
    hc = cw[0] * hp[:, 0:L] + cw[1] * hp[:, 1:L + 1] + cw[2] * hp[:, 2:L + 2] + lw['conv_b']
    hg, hu = jnp.split(hc, 2, axis=-1)
    x = x + (jax.nn.silu(hg) * hu) @ lw['w_down']
    conv_state = hp[:, -(CONV_W - 1):]
    ple_gate = jax.nn.sigmoid(rmsnorm(x, lw['ple_norm_w']) @ lw['w_ple_gate'])
    x = x + ple_gate * (p @ lw['w_ple_proj'])
    return x, k_new, v_new, sgu_state, conv_state


def setup_inputs(seed: int = 0) -> dict:
    key = jax.random.key(seed)
    ks = iter(jax.random.split(key, 40))

    def nrm(shape, scale=1.0):
        return jax.random.normal(next(ks), shape, jnp.float32) * scale

    def gain(shape):
        return 1.0 + nrm(shape, 0.05)

    return {
        'x_prompt': nrm((BATCH, SEQ, D_MODEL)),
        'x_sample': nrm((DEC_BATCH, DEC_SEQ, D_MODEL)),
        'p_prompt': nrm((DEPTH, BATCH, SEQ, PLE_DIM)),
        'p_sample': nrm((DEPTH, DEC_BATCH, DEC_SEQ, PLE_DIM)),
        'state_attn_k': nrm((DEPTH, DEC_BATCH, WINDOW, N_KV_HEADS, HEAD_DIM)),
        'state_attn_v': nrm((DEPTH, DEC_BATCH, WINDOW, N_KV_HEADS, HEAD_DIM)),
        'state_conv': nrm((DEPTH, DEC_BATCH, CONV_W - 1, 2 * D_FF)),
        'attn_norm_w': gain((DEPTH, D_MODEL)),
        'w_in': nrm((DEPTH, D_MODEL, IN_W), D_MODEL ** -0.5),
        'q_norm_w': gain((DEPTH, HEAD_DIM)),
        'k_norm_w': gain((DEPTH, HEAD_DIM)),
        'attn_sinks': nrm((DEPTH, N_HEADS), 0.5),
        'sgu_norm_w': gain((DEPTH, GM_W)),
        'sgu_w': nrm((DEPTH, GM_GROUPS, CHUNK, CHUNK), 0.5 * CHUNK ** -0.5),
        'sgu_b': 1.0 + nrm((DEPTH, GM_GROUPS, CHUNK), 0.02),
        'w_br_attn': nrm((DEPTH, ATTN_W, D_MODEL), ATTN_W ** -0.5),
        'w_br_gm': nrm((DEPTH, GM_W, D_MODEL), GM_W ** -0.5),
        'w_gate': nrm((DEPTH, D_MODEL, 2 * D_MODEL), D_MODEL ** -0.5),
        'b_gate': nrm((DEPTH, 2 * D_MODEL), 0.02),
        'w_out': nrm((DEPTH, D_MODEL, D_MODEL), D_MODEL ** -0.5),
        'ffn_norm_w': gain((DEPTH, D_MODEL)),
        'w_up': nrm((DEPTH, D_MODEL, 2 * D_FF), D_MODEL ** -0.5),
        'conv_w': nrm((DEPTH, CONV_W, 2 * D_FF), CONV_W ** -0.5),
        'conv_b': nrm((DEPTH, 2 * D_FF), 0.02),
        'w_down': nrm((DEPTH, D_FF, D_MODEL), D_FF ** -0.5),
        'ple_norm_w': gain((DEPTH, D_MODEL)),
        'w_ple_gate': nrm((DEPTH, D_MODEL, D_MODEL), D_MODEL ** -0.5),
        'w_ple_proj': nrm((DEPTH, PLE_DIM, D_MODEL), PLE_DIM ** -0.5),
    }


def reference(x_prompt, x_sample, p_prompt, p_sample, state_attn_k, state_attn_v, state_conv,
              attn_norm_w, w_in, q_norm_w, k_norm_w, attn_sinks, sgu_norm_w, sgu_w, sgu_b,
              w_br_attn, w_br_gm, w_gate, b_gate, w_out, ffn_norm_w, w_up, conv_w, conv_b,
              w_down, ple_norm_w, w_ple_gate, w_ple_proj):
    xp, xs = x_prompt, x_sample
    kp_l, vp_l, ks_l, vs_l, gp_l, gs_l, cp_l, cs_l = [], [], [], [], [], [], [], []
    for i in range(DEPTH):
        lw = {
            'attn_norm_w': attn_norm_w[i], 'w_in': w_in[i], 'q_norm_w': q_norm_w[i],
            'k_norm_w': k_norm_w[i], 'attn_sinks': attn_sinks[i], 'sgu_norm_w': sgu_norm_w[i],
            'sgu_w': sgu_w[i], 'sgu_b': sgu_b[i], 'w_br_attn': w_br_attn[i], 'w_br_gm': w_br_gm[i],
            'w_gate': w_gate[i], 'b_gate': b_gate[i], 'w_out': w_out[i], 'ffn_norm_w': ffn_norm_w[i],
            'w_up': w_up[i], 'conv_w': conv_w[i], 'conv_b': conv_b[i], 'w_down': w_down[i],
            'ple_norm_w': ple_norm_w[i], 'w_ple_gate': w_ple_gate[i], 'w_ple_proj': w_ple_proj[i],
        }
        xp, kp, vp, gp, cp = layer(xp, p_prompt[i], None, None, None, lw, True)
        xs, ksn, vsn, gs, cs = layer(xs, p_sample[i], state_attn_k[i], state_attn_v[i], state_conv[i], lw, False)
        kp_l.append(kp); vp_l.append(vp); ks_l.append(ksn); vs_l.append(vsn)
        gp_l.append(gp); gs_l.append(gs); cp_l.append(cp); cs_l.append(cs)
    attn_k_prompt = jnp.stack(kp_l)
    attn_v_prompt = jnp.stack(vp_l)
    attn_k_sample = jnp.stack(ks_l)
    attn_v_sample = jnp.stack(vs_l)
    sgu_v_prompt = jnp.stack(gp_l)
    sgu_v_sample = jnp.stack(gs_l)
    conv_prompt = jnp.stack(cp_l)
    conv_sample = jnp.stack(cs_l)
    return (xp, xs, attn_k_prompt, attn_v_prompt, attn_k_sample, attn_v_sample, sgu_v_prompt, sgu_v_sample, conv_prompt, conv_sample)
```

```python
import contextlib
import numpy as np
import concourse.bass as bass
import concourse.mybir as mybir
from concourse.bass_utils import run_bass_kernel_spmd

F32 = mybir.dt.float32
BF16 = mybir.dt.bfloat16
AF = mybir.ActivationFunctionType
ALU = mybir.AluOpType
AX = mybir.AxisListType

D = 2048
NJ = 2
NMAIN = 512
CM = 516
DFF = 5632
NH = 16
EPS = 1e-6
ENGS = ("pe", "act", "dve", "pool", "sp")
ARENA_KEYS = {"xTh", "xnh", "qT", "kT", "kTh", "kTs", "vdup", "vduph", "vdups", "gvn", "gvnh", "gvns", "gvf", "gvo",
              "kdup", "qn", "knf", "vf", "ebuf", "pbuf", "mg", "gsb", "actb", "hbuf", "cbuf", "sgb", "sgb2", "wppb"}


class Op:
    __slots__ = ("eng", "fn", "deps", "idx", "gidx", "signal", "sigval", "dsem", "dval", "clock")

    def __init__(self, eng, fn):
        self.eng = eng
        self.fn = fn
        self.deps = []
        self.signal = False
        self.sigval = 0
        self.dsem = None
        self.dval = 0
        self.clock = None


class Sched:
    def __init__(self):
        self.ops = {e: [] for e in ENGS}
        self.all = []
        self.last_w = {}
        self.readers = {}
        self.dma_cnt = {}

    maxops = 10 ** 9

    def add(self, eng, fn, reads=(), writes=(), dsem=None):
        if len(self.all) >= self.maxops:
            return None
        reads = list(reads)
        exp_ = []
        for k in reads:
            if isinstance(k, tuple) and len(k) == 2 and k[0] == "wsl":
                exp_ += [("wsl", k[1], q_) for q_ in range(4)]
            else:
                exp_.append(k)
        reads = exp_
        for k in reads + list(writes):
            kn_ = k[0] if isinstance(k, tuple) else k
            if kn_ in ARENA_KEYS:
                reads.append("ARENA")
                break
        op = Op(eng, fn)
        deps = {}
        for k in list(reads) + list(writes):
            w = self.last_w.get(k)
            if w is not None:
                deps[id(w)] = w
        for k in writes:
            for r in self.readers.get(k, ()):
                deps[id(r)] = r
        for k in reads:
            if isinstance(k, tuple) and k[0] == "ps":
                for r in self.readers.get(k, ()):
                    if r.eng != eng:
                        deps[id(r)] = r
        for d in deps.values():
            if d.dsem is None and d.eng == eng and eng == "pe":
                continue
            op.deps.append(d)
        for k in writes:
            self.last_w[k] = op
            self.readers[k] = []
        for k in reads:
            self.readers.setdefault(k, []).append(op)
        if dsem is not None:
            op.dsem = dsem
            self.dma_cnt[dsem] = self.dma_cnt.get(dsem, 0) + 16
            op.dval = self.dma_cnt[dsem]
        op.idx = len(self.ops[eng])
        op.gidx = len(self.all)
        self.ops[eng].append(op)
        self.all.append(op)
        return op

    def finalize(self):
        for op in self.all:
            for d in op.deps:
                if d.dsem is None:
                    d.signal = True
        for e in ENGS:
            c = 0
            for op in self.ops[e]:
                if op.dsem is None and op.signal:
                    c += 1
                    op.sigval = c
        known = {e: {} for e in ENGS}
        self.waits = {}
        for op in self.all:
            kn = known[op.eng]
            wm = {}
            for d in sorted(op.deps, key=lambda d: -d.gidx):
                if d.dsem is not None:
                    key, val = ("D", d.dsem), d.dval
                else:
                    key, val = ("E", d.eng), d.sigval
                if kn.get(key, 0) >= val:
                    continue
                wm[key] = max(wm.get(key, 0), val)
                for k2, v2 in d.clock.items():
                    if kn.get(k2, 0) < v2:
                        kn[k2] = v2
                kn[key] = max(kn.get(key, 0), val)
            self.waits[op.gidx] = list(wm.items())
            clk = dict(kn)
            if op.dsem is not None:
                clk[("D", op.dsem)] = max(clk.get(("D", op.dsem), 0), op.dval)
            elif op.signal:
                clk[("E", op.eng)] = max(clk.get(("E", op.eng), 0), op.sigval)
            op.clock = clk

    def emit(self, nc):
        self.finalize()
        dsems = sorted(self.dma_cnt.keys())
        with contextlib.ExitStack() as es:
            sems = {}
            for e in ENGS:
                sems[("E", e)] = es.enter_context(nc.semaphore("s_" + e))
            for d in dsems:
                sems[("D", d)] = es.enter_context(nc.semaphore("d_" + d))
            block = es.enter_context(nc.Block())

            def run(engname, eng):
                for op in self.ops[engname]:
                    for key, val in self.waits[op.gidx]:
                        eng.wait_ge(sems[key], val)
                    ins = op.fn(eng)
                    if op.dsem is not None:
                        ins.then_inc(sems[("D", op.dsem)], 16)
                    elif op.signal:
                        ins.then_inc(sems[("E", engname)], 1)
                if engname == "sp":
                    for d in dsems:
                        eng.wait_ge(sems[("D", d)], self.dma_cnt[d])
                    for e2 in ENGS:
                        tot = max([o.sigval for o in self.ops[e2] if o.dsem is None] + [0])
                        if tot:
                            eng.wait_ge(sems[("E", e2)], tot)

            block.tensor(lambda pe: run("pe", pe))
            block.scalar(lambda a: run("act", a))
            block.vector(lambda v: run("dve", v))
            block.gpsimd(lambda g: run("pool", g))
            block.sync(lambda s: run("sp", s))


class _Stop(Exception):
    pass


def build_program(nj=NJ, stop=99):
    nc = bass.Bass("TRN2", target_bir_lowering=False)

    def ckpt(n):
        if n >= stop:
            raise _Stop()

    def din(name, shape):
        return nc.dram_tensor(name, list(shape), F32, kind="ExternalInput").ap()

    def dout(name, shape):
        return nc.dram_tensor(name, list(shape), F32, kind="ExternalOutput").ap()

    xm = din("xm", [nj, 512, D]); xh = din("xh", [nj, 256, D]); xs = din("xs", [nj, 4, D])
    pm = din("pm", [nj, 516, 256])
    sk = din("sk", [nj, 2, 128, 256]); sv = din("sv", [nj, 2, 128, 256])
    sc = din("sc", [nj, 4, 2 * DFF])
    hvb_d = din("hvb", [nj, 128, 1])
    maskab_d = din("maskab", [128, 2 * NH * 128]); masks_d = din("masks", [128, 5 * NH * 4])
    ident_d = din("ident", [128, 128]); tri_d = din("tri", [128, 128])
    w_in = din("w_in", [D, 3584]); w_bra = din("w_br_attn", [1024, D]); w_brg = din("w_br_gm", [1024, D])
    w_gate = din("w_gate", [D, 2 * D]); w_out = din("w_out", [D, D]); w_up = din("w_up", [D, 2 * DFF])
    w_down = din("w_down", [DFF, D]); w_pg = din("w_ple_gate", [D, D]); w_pp = din("w_ple_proj", [256, D])
    nw1_d = din("attn_norm_w", [D]); nw2_d = din("ffn_norm_w", [D]); nw3_d = din("ple_norm_w", [D])
    qnw_d = din("q_norm_w", [64]); knw_d = din("k_norm_w", [64]); sink_d = din("attn_sinks", [NH])
    snw_d = din("sgu_norm_w", [1024]); sguw_d = din("sgu_w", [4, 128, 128]); sgub_d = din("sgu_b", [512])
    bg_d = din("b_gate", [2 * D]); cw_d = din("conv_w", [3, 2 * DFF]); cb_d = din("conv_b", [2 * DFF])
    ym = dout("ym", [nj, 512, D]); ys = dout("ys", [nj, 2, D])
    ko = dout("ko", [nj, 128, 256]); vo = dout("vo", [nj, 128, 256])
    kso = dout("kso", [nj, 2, 128, 256]); vso = dout("vso", [nj, 2, 128, 256])
    go = dout("go", [nj, 128, 1024]); gso = dout("gso", [nj, 2, 1024])
    co = dout("co", [nj, 2, 2 * DFF]); cso = dout("cso", [nj, 2, 2, 2 * DFF])

    S = Sched()
    import os as _os
    S.maxops = int(_os.environ.get("MAXOPS", 10 ** 9))
    es = contextlib.ExitStack()

    def sb(name, shape, dt=F32):
        return es.enter_context(nc.sbuf_tensor("sb_" + name, list(shape), dt))

    xT = sb("xT", [128, 16, CM]); xnT = sb("xnT", [128, 16, CM], BF16)
    pT = sb("pT", [128, 2, CM], BF16)
    xtm = [sb(f"xtm{i}", [128, 1024]) for i in range(2)]
    ytm = xtm
    wsl = [sb(f"wsl{i}", [128, 16, 512], BF16) for i in range(2)]
    sqb = sb("sqb", [128, 2, 512], BF16)
    guT = sb("guT", [128, 8, CM], BF16); attnT = sb("attnT", [128, 8, CM], BF16)
    scr = [sb(f"scr{i}", [128, 512]) for i in range(3)]
    scr.append(scr[1])
    rstd = scr[2]
    sstat = sb("sstat", [128, 64])
    hS = sb("hS", [128, 88, 2]); hlast = sb("hlast", [128, 88, 2]); stT = sb("stT", [128, 88, 4])
    cS = sb("cS", [128, 2, 2]); dummy = sb("fdummy", [128, 2]); hvb = sb("hvbs", [128, 1])
    AW = 17470
    arena = sb("arena", [128, AW])

    def carver():
        off = [0]

        def carve(shape, dt=F32):
            n = int(np.prod(shape[1:]))
            nw_ = n if dt == F32 else (n + 1) // 2
            assert off[0] + nw_ <= AW, (off[0], nw_)
            ap = arena[:, off[0]:off[0] + nw_]
            off[0] += nw_
            if dt != F32:
                ap = ap.bitcast(dt)[:, 0:n]
            if len(shape) == 3:
                ap = ap.rearrange("p (a b) -> p a b", a=shape[1])
            return ap
        return carve
    ca = carver()
    xTh_raw = ca([128, 2048])
    xTh = xTh_raw.rearrange("p (a b) -> p a b", a=16)
    xnh = ca([128, 16, 256], BF16)
    qT = ca([128, 8, CM], BF16); kT = ca([128, 4, CM], BF16); kTh = ca([128, 4, 256], BF16)
    kTs = ca([128, 8, 128], BF16).rearrange("p (s g) k -> p s g k", s=2)
    vdup = ca([128, 5, 512], BF16); vduph = ca([128, 2, 512], BF16); vdups = ca([128, 2, 512], BF16)
    gvn = [ca([128, 1024], BF16) for i in range(2)]
    gvnh = ca([128, 1024], BF16); gvns = ca([128, 1024], BF16)
    gvf = [xTh_raw[:, 0:1024]] * 2
    gvo = xTh_raw[:, 1024:2048]
    kdup2 = [ca([128, 512], BF16) for i in range(2)]; qn2 = [ca([128, 512], BF16) for i in range(2)]
    kdup = kdup2[0]; qn = qn2[0]
    knf = ca([128, 256]); vf = ca([128, 256])
    ebuf = [ca([128, 512], BF16) for i in range(2)]
    pbuf = [ca([128, 512], BF16) for i in range(2)]
    pbuf2 = [pbuf, [ca([128, 512], BF16) for i in range(2)], [ca([128, 512], BF16) for i in range(2)]]
    cb2 = carver()
    mergedT = cb2([128, 16, CM], BF16)
    gsb = [cb2([128, 4, CM], BF16) for i in range(2)]
    actb = [cb2([128, 11, CM], BF16) for i in range(2)]
    hbuf = [cb2([128, 516]) for i in range(2)]
    cbuf = [cb2([128, 512]) for i in range(2)]
    sgb = cb2([128, 516])
    wppb = cb2([128, 2, D], BF16)
    maskab = sb("maskab", [128, 2, NH, 128], BF16)
    masks = sb("masks", [128, 5, NH, 4], BF16)
    ident = sb("ident", [128, 128]); identb = sb("identb", [128, 128], BF16); tri = sb("tri", [128, 128])
    onesn = sb("onesn", [128, 128], BF16); ones1 = sb("ones1", [128, 128], BF16)
    epsb = sb("epsb", [128, 1])
    nw = [sb(f"nw{i}", [128, 16]) for i in range(3)]
    qnw = sb("qnw", [128, 64]); knw = sb("knw", [128, 64]); sinkb = sb("sinkb", [128, NH]); sinkx = sb("sinkx", [128, 4, 2, 2])
    sxh = sb("sxh", [1, 4, 2, 2], BF16); sxl = sb("sxl", [1, 4, 2, 2], BF16); sxt = sb("sxt", [1, 4, 2, 2])
    snw = sb("snw", [128, 1024]); wsT = sb("wsT", [128, 4, 128], BF16)
    bh = sb("bh", [1, 512], BF16); bl = sb("bl", [1, 512], BF16)
    w00 = sb("w00", [2, 4]); w00I = sb("w00I", [2, 4, 2], BF16)
    bgT = sb("bgT", [128, 32]); cwT = sb("cwT", [128, 3, 88]); cbT = sb("cbT", [128, 88])
    ps = [es.enter_context(nc.psum_tensor(f"ps{i}", [128, 512], F32)) for i in range(8)]

    bankc = [0]

    def bank():
        b = bankc[0] % 8
        bankc[0] += 1
        return b

    def PK(b):
        return ("ps", b)

    def mm(out, lhsT, rhs, start, stop, reads, writes):
        S.add("pe", lambda e: e.matmul(out=out, lhsT=lhsT, rhs=rhs, start=start, stop=stop), reads, writes)

    def tr(out, in_, idn, reads, writes):
        S.add("pe", lambda e: e.transpose(out=out, in_=in_, identity=idn), reads, writes)

    def act(out, in_, func, reads, writes, bias=None, scale=None, accum=None):
        kw = {}
        if bias is not None:
            kw["bias"] = bias
        if scale is not None:
            kw["scale"] = scale
        if accum is not None:
            kw["accum_out"] = accum
        S.add("act", lambda e: e.activation(out=out, in_=in_, func=func, **kw), reads, writes)

    def tt(out, in0, in1, op, reads, writes, eng="dve"):
        S.add(eng, lambda e: e.tensor_tensor(out=out, in0=in0, in1=in1, op=op), reads, writes)

    def ts(out, in0, s1, s2, op0, op1, reads, writes, eng="dve"):
        if s2 is None:
            S.add(eng, lambda e: e.tensor_scalar(out=out, in0=in0, scalar1=s1, scalar2=None, op0=op0), reads, writes)
        else:
            S.add(eng, lambda e: e.tensor_scalar(out=out, in0=in0, scalar1=s1, scalar2=s2, op0=op0, op1=op1), reads, writes)

    def stt(out, in0, scalar, in1, op0, op1, reads, writes, eng="dve"):
        S.add(eng, lambda e: e.scalar_tensor_tensor(out=out, in0=in0, scalar=scalar, in1=in1, op0=op0, op1=op1), reads, writes)

    def cp(out, in_, reads, writes, eng="dve"):
        if eng == "act":
            act(out, in_, AF.Copy, reads, writes)
        else:
            S.add(eng, lambda e: e.tensor_copy(out=out, in_=in_), reads, writes)

    def recip(ap, key):
        S.add("dve", lambda e: e.reciprocal(out=ap, in_=ap), [key], [key])

    def dma(eng, out, in_, reads, writes, dsem, slow=False):
        if slow:
            S.add(eng, lambda e: e.dma_start(out=out, in_=in_, allow_slow_non_contiguous=True), reads, writes, dsem=dsem)
        else:
            S.add(eng, lambda e: e.dma_start(out=out, in_=in_), reads, writes, dsem=dsem)

    cdc = [0]

    const_ops = []

    def cdma(out, in_, writes, slow=False, eng="sp"):
        cdc[0] += 1
        dma(eng, out, in_, [], list(writes) + [("cthr", cdc[0] % 4)], f"k{cdc[0]}", slow)

    cdma(ident[:], ident_d[:, :], ["ident"]); cdma(tri[:], tri_d[:, :], ["tri"])
    vslot = [0]

    def vecT(src, n, dst, dkey):
        o = (vslot[0] % 8) * 128
        vslot[0] += 1
        vstage = xtm[1]
        cdma(vstage[0:n, o:o + 128], src.rearrange("(c p) -> c p", p=128), [("xtm", 1)])
        b = bank()
        tr(ps[b][:, 0:n], vstage[0:n, o:o + 128], ident[0:n, 0:n], [("xtm", 1), "ident"], [PK(b)])
        cp(dst, ps[b][:, 0:n], [PK(b)], [dkey])

    for i, d_ in enumerate((nw1_d, nw2_d, nw3_d)):
        vecT(d_, 16, nw[i][:], ("nw", i))
    cdma(qnw[:], qnw_d.partition_broadcast(128), ["qnw"]); cdma(knw[:], knw_d.partition_broadcast(128), ["knw"])
    cdma(sinkb[:], sink_d.partition_broadcast(128), ["sinkb"]); cdma(snw[:], snw_d.partition_broadcast(128), ["snw"])
    bfull = scr[0][0:1, :]
    btmp = scr[1][0:1, :]
    cdma(bfull, sgub_d.partition_broadcast(1), [("scr", 0)])
    cdma(w00[:], sguw_d[:, 0, 0].partition_broadcast(2), ["w00"], slow=True)
    vecT(bg_d, 32, bgT[:], "bgT")
    vecT(cb_d, 88, cbT[:], "cbT")
    for k_ in range(3):
        vecT(cw_d[k_], 88, cwT[:, k_, :], "cwT")
    S.add("pool", lambda e: e.dma_start(out=maskab[:], in_=maskab_d.rearrange("p (r h q) -> p r h q", r=2, h=NH), max_dma_last_dim=2048),
          [], ["maskab"], dsem="cm0")
    dma("pool", masks[:], masks_d.rearrange("p (k h q) -> p k h q", k=5, h=NH), [], ["masks"], "cm1")
    S.add("dve", lambda e: e.memset(onesn[:], 1.0 / 2048.0), [], ["onesn"])
    S.add("dve", lambda e: e.memset(ones1[:], 1.0), [], ["ones1"])
    S.add("dve", lambda e: e.memset(epsb[:], EPS), [], ["epsb"])
    cp(identb[:], ident[:], ["ident"], ["identb"])
    act(sinkb[:], sinkb[:], AF.Exp, ["sinkb"], ["sinkb"])
    cp(sinkx[:], sinkb[:, :].rearrange("p (g h2 par) -> p g par h2", g=4, h2=2, par=2), ["sinkb"], ["sinkx"])
    cp(sxh[:], sinkx[0:1], ["sinkx"], ["sxh"])
    tt(sxt[:], sinkx[0:1], sxh[:], ALU.subtract, ["sinkx", "sxh"], ["sxt"])
    cp(sxl[:], sxt[:], ["sxt"], ["sxl"])
    cp(bh[:], bfull, [("scr", 0)], ["bh"])
    tt(btmp, bfull, bh[:], ALU.subtract, [("scr", 0), "bh"], [("scr", 1)])
    cp(bl[:], btmp, [("scr", 1)], ["bl"])
    for g in range(4):
        ts(w00I[:, g, :], ident[0:2, 0:2], w00[0:2, g:g + 1], None, ALU.mult, None, ["ident", "w00"], [("w00I", g)])
    for g in range(4):
        b = bank()
        cdma(xtm[0][:, g * 128:(g + 1) * 128], sguw_d[g], [("xtm", 0)])
        tr(ps[b][:, 0:128], xtm[0][:, g * 128:(g + 1) * 128], ident[:], [("xtm", 0), "ident"], [PK(b)])
        tt(wsT[:, g, :], ps[b][:, 0:128], tri[:], ALU.mult, [PK(b), "tri"], [("wsT", g)])
    WST = [("wsT", g) for g in range(4)]

    if _os.environ.get("DBGV", "") == "early":
        for _i in range(int(_os.environ.get("NEARLY", "1"))):
            dma("sp", ko[0, :, 0:128], ident[:], ["ident"], [], "oK")
    wcnt = [0]

    def wload(parts, keys_extra=()):
        s = wcnt[0] % 2
        wcnt[0] += 1
        for (fn, src, (cc0, cc1)) in parts:
            wk = [("wsl", s, q_) for q_ in range(cc0 // 128, (cc1 - 1) // 128 + 1)]
            dma("pool", fn(wsl[s]), src, [], wk, f"w{s}")
        return s

    def tm2fm(src, nrows, ncols, evac, slot, koff=0):
        st = xtm[slot]
        dma("sp", st[0:nrows, 0:ncols], src, [], [("xtm", slot)], f"xin{slot}")
        nch = ncols // 128
        per = max(1, min(4, 512 // nrows)) if nrows >= 128 else 16
        for q0 in range(0, nch, per):
            n = min(per, nch - q0)
            b = bank()
            for i in range(n):
                tr(ps[b][:, i * nrows:(i + 1) * nrows], st[0:nrows, (q0 + i) * 128:(q0 + i + 1) * 128], ident[0:nrows, 0:nrows],
                   [("xtm", slot), "ident"], [PK(b)])
            evac(ps[b][:, 0:n * nrows].rearrange("p (a r) -> p a r", a=n), q0 + koff, n, PK(b))

    def norm_fm(xbuf, xkey, cols, nwi, obuf, okey, ocol=None):
        c0, c1 = cols
        n = c1 - c0
        oc0 = c0 if ocol is None else ocol
        b = None
        for kc in range(16):
            act(sqb[:, kc % 2, 0:n], xbuf[:, kc, c0:c1], AF.Square, [xkey(kc)], [("sqb", kc % 2)])
            if kc == 0:
                b = bank()
            mm(ps[b][:, 0:n], onesn[:], sqb[:, kc % 2, 0:n], kc == 0, kc == 15, ["onesn", ("sqb", kc % 2)], [PK(b)])
        act(rstd[:, 0:n], ps[b][:, 0:n], AF.Sqrt, [PK(b), "epsb"], [("scr", 2)], bias=epsb[:], scale=1.0)
        recip(rstd[:, 0:n], ("scr", 2))
        for kc in range(16):
            stt(obuf[:, kc, oc0:oc0 + n], xbuf[:, kc, c0:c1], nw[nwi][:, kc:kc + 1], rstd[:, 0:n], ALU.mult, ALU.mult,
                [xkey(kc), ("nw", nwi), ("scr", 2)], [(okey[0][0], kc)])

    TILES = (("a", 0, 258), ("b", 258, 516))

    def pipe(units, depth=1):
        n = len(units)
        for i in range(min(depth, n)):
            units[i][0]()
        for i in range(n):
            if i + depth < n:
                units[i + depth][0]()
            units[i][1]()

    def K(name, t):
        return [(name, t)]

    pending = []

    def step():
        if pending:
            pending.pop(0)()

    def fm_group(nmc, kparts, evac, accum_keys):
        for mc in range(nmc):
            step()
            for (tn, c0, c1) in TILES:
                b = bank()
                kp = kparts(mc)
                for i, (lh, rf, rd) in enumerate(kp):
                    mm(ps[b][:, 0:c1 - c0], lh, rf(c0, c1), i == 0, i == len(kp) - 1, rd(tn), [PK(b)])
                evac(mc, tn, c0, c1, ps[b][:, 0:c1 - c0], PK(b))

    for j in range(nj):
      try:
        ckpt(0)
        S.add("dve", lambda e: e.memset(dummy[:], 0.0), [], ["ARENA"])
        dma("sp", hvb[:], hvb_d[j], [], ["hvb"], "c1")
        slot = [0]

        def nslot():
            slot[0] += 1
            return slot[0] % 2

        for hb in range(2):
            def ev(pa, q0, n, pk, hb=hb):
                cp(xTh[:, q0:q0 + n, :], pa, [pk], [("xTh", c_) for c_ in range(q0, q0 + n)], eng="act")
            for hf in range(2):
                tm2fm(xh[j, hb * 128:(hb + 1) * 128, hf * 1024:(hf + 1) * 1024], 128, 1024, ev, nslot(), koff=8 * hf)
            norm_fm(xTh, (lambda kc: ("xTh", kc)), (0, 128), 0, xnh, [("xnh",)], ocol=hb * 128)
        for blk in range(4):
            def ev(pa, q0, n, pk, blk=blk):
                cp(xT[:, q0:q0 + n, blk * 128:(blk + 1) * 128], pa, [pk], [("xT", c_) for c_ in range(q0, q0 + n)], eng="act")
            for hf in range(2):
                tm2fm(xm[j, blk * 128:(blk + 1) * 128, hf * 1024:(hf + 1) * 1024], 128, 1024, ev, nslot(), koff=8 * hf)
        for hf in range(2):
            tm2fm(xs[j, :, hf * 1024:(hf + 1) * 1024], 4, 1024,
                  lambda pa, q0, n, pk: cp(xT[:, q0:q0 + n, 512:516], pa, [pk], [("xT", c_) for c_ in range(q0, q0 + n)], eng="act"), nslot(), koff=8 * hf)
        def deferred_state():
            for s_ in range(2):
                sl_ = nslot()
                dma("sp", xtm[sl_][:, 0:256], sk[j, s_], [], [("xtm", sl_)], f"xin{sl_}")
                dma("sp", xtm[sl_][:, 256:512], sv[j, s_], [], [("xtm", sl_)], f"xin{sl_}")
                cp(kdup[:].rearrange("p (g t d) -> p g t d", g=4, t=2),
                   xtm[sl_][:, 0:256].rearrange("p (g d) -> p g d", g=4).unsqueeze(2).broadcast_to([128, 4, 2, 64]),
                   [("xtm", sl_)], [("kdup", 0)])
                cp(vdups[:, s_, :].rearrange("p (g t d) -> p g t d", g=4, t=2),
                   xtm[sl_][:, 256:512].rearrange("p (g d) -> p g d", g=4).unsqueeze(2).broadcast_to([128, 4, 2, 64]),
                   [("xtm", sl_)], [("vdups",)], eng="act")
                b = bank()
                pb = ps[b][:].bitcast(BF16)
                for g in range(4):
                    tr(pb[:, g * 128:(g + 1) * 128], kdup[:, g * 128:(g + 1) * 128], identb[:], [("kdup", 0), "identb"], [PK(b)])
                cp(kTs[:, s_, :, :], pb[:, 0:512].rearrange("p (g k) -> p g k", g=4), [PK(b)], [("kTs",)], eng="act")


        def deferred_p_sc():
            for blk in range(4):
                def ev(pa, q0, n, pk, blk=blk):
                    cp(pT[:, q0:q0 + n, blk * 128:(blk + 1) * 128], pa, [pk], [("pT",)], eng="dve")
                pending.append(lambda ev=ev, blk=blk: tm2fm(pm[j, blk * 128:(blk + 1) * 128, :], 128, 256, ev, nslot()))
            pending.append(lambda: tm2fm(pm[j, 512:516, :], 4, 256, lambda pa, q0, n, pk: cp(pT[:, q0:q0 + n, 512:516], pa, [pk], [("pT",)], eng="dve"), nslot()))
            for pc in range(0, 88, 8):
                n_ = min(8, 88 - pc)
                def ev(pa, q0, n, pk, pc=pc):
                    cp(stT[:, pc + q0:pc + q0 + n, :], pa, [pk], [("stT",)], eng="dve")
                pending.append(lambda ev=ev, pc=pc, n_=n_: tm2fm(sc[j, :, pc * 128:(pc + n_) * 128], 4, n_ * 128, ev, nslot()))

        XM = lambda kc: ("xT", kc)
        for (_tn, _c0, _c1) in TILES:
            norm_fm(xT, XM, (_c0, _c1), 0, xnT, [("xn",)])

        ckpt(1)
        def win(c0):
            return wload([((lambda w, h_=h_: w[:, :, h_ * 256:(h_ + 1) * 256]),
                           w_in[:, c0 + h_ * 256:c0 + (h_ + 1) * 256].rearrange("(kc p) n -> p kc n", p=128),
                           (h_ * 256, (h_ + 1) * 256)) for h_ in range(2)])

        MB = [("m%d" % i, xnT, i * 128, 128, [("xn",)]) for i in range(4)]
        SBk = ("s", xnT, 512, 4, [("xn",)])
        HB = [("h%d" % i, xnh, i * 128, 128, [("xnh",)]) for i in range(2)]

        def tm_mm(blk, s, b):
            _, xb, c0, np_, xk = blk
            for kc in range(16):
                mm(ps[b][0:np_, :], xb[:, kc, c0:c0 + np_], wsl[s][:, kc, :], kc == 0, kc == 15, [(xk[0][0], kc), ("wsl", s)], [PK(b)])

        def headnorm(src, np_, nh_, wt, wkey, dst, dkey, srckey, sidx):
            sc_ = scr[sidx]
            act(sc_[0:np_, 0:nh_ * 64], src, AF.Square, srckey, [("scr", sidx)])
            S.add("dve", lambda e: e.tensor_reduce(out=sstat[0:np_, 0:nh_], in_=sc_[0:np_, 0:nh_ * 64].rearrange("p (h d) -> p h d", h=nh_),
                                                     axis=AX.X, op=ALU.add), [("scr", sidx)], ["sstat"])
            act(sstat[0:np_, 0:nh_], sstat[0:np_, 0:nh_], AF.Sqrt, ["sstat", "epsb"], ["sstat"], bias=epsb[0:np_, :], scale=1.0 / 64.0)
            recip(sstat[0:np_, 0:nh_], "sstat")
            tt(sc_[0:np_, 0:nh_ * 64].rearrange("p (h d) -> p h d", h=nh_), src.rearrange("p (h d) -> p h d", h=nh_),
               sstat[0:np_, 0:nh_].unsqueeze(2).broadcast_to([np_, nh_, 64]), ALU.mult, srckey + ["sstat"], [("scr", sidx)])
            tt(dst.rearrange("p (h d) -> p h d", h=nh_), sc_[0:np_, 0:nh_ * 64].rearrange("p (h d) -> p h d", h=nh_),
               wt[0:np_, :].unsqueeze(1).broadcast_to([np_, nh_, 64]), ALU.mult, [("scr", sidx), wkey], dkey)

        s = win(1024)
        kv_units = []
        for bi, blk in enumerate(HB + MB + [SBk]):
            st_ = {}

            def pre(bi=bi, blk=blk, st_=st_):
                name, xb, c0, np_, xk = blk
                kd_ = kdup2[bi % 2]
                KD = ("kdup", bi % 2)
                b = bank()
                tm_mm(blk, s, b)
                headnorm(ps[b][0:np_, 0:256], np_, 4, knw, "knw", knf[0:np_, :], ["knf"], [PK(b)], 0)
                cp(kd_[0:np_, :].rearrange("p (g t d) -> p g t d", g=4, t=2),
                   knf[0:np_, :].rearrange("p (g d) -> p g d", g=4).unsqueeze(2).broadcast_to([np_, 4, 2, 64]), ["knf"], [KD])
                if name.startswith("h"):
                    vd, vkey = vduph[:, int(name[1]), :], ("vduph",)
                elif name == "s":
                    vd, vkey = vdup[:, 4, :], ("vdup", 4)
                else:
                    vd, vkey = vdup[:, int(name[1]), :], ("vdup", int(name[1]))
                cp(vd[0:np_, :].rearrange("p (g t d) -> p g t d", g=4, t=2),
                   ps[b][0:np_, 256:512].rearrange("p (g d) -> p g d", g=4).unsqueeze(2).broadcast_to([np_, 4, 2, 64]),
                   [PK(b)], [vkey], eng="act")
                if name == "m3" or name == "s":
                    cp(vf[0:np_, :], ps[b][0:np_, 256:512], [PK(b)], ["vf"], eng="dve")
                    if name == "m3":
                        dma("sp", ko[j], knf[:], ["knf"], [], "oK")
                        dma("sp", vo[j], vf[:], ["vf"], [], "oV")
                    else:
                        dma("sp", kso[j, :, 127, :], knf[0:2, :], ["knf"], [], "oK"); dma("sp", vso[j, :, 127, :], vf[0:2, :], ["vf"], [], "oV")

            def post(bi=bi, blk=blk):
                name, xb, c0, np_, xk = blk
                kd_ = kdup2[bi % 2]
                KD = ("kdup", bi % 2)
                b2 = bank()
                pb = ps[b2][:].bitcast(BF16)
                for g in range(4):
                    tr(pb[:, g * np_:(g + 1) * np_], kd_[0:np_, g * 128:(g + 1) * 128], identb[0:np_, 0:np_], [KD, "identb"], [PK(b2)])
                src_ = pb[:, 0:4 * np_].rearrange("p (g k) -> p g k", g=4)
                if name.startswith("h"):
                    cp(kTh[:, :, c0:c0 + np_], src_, [PK(b2)], [("kTh",)], eng="act")
                else:
                    cp(kT[:, :, c0:c0 + np_], src_, [PK(b2)], [("kT", name)], eng="act")
            kv_units.append((pre, post))
        pipe(kv_units)

        ckpt(2)
        deferred_state()
        for half in range(2):
            s = win(half * 512)
            q_units = []
            for bi, blk in enumerate(MB + [SBk]):
                def pre(bi=bi, blk=blk, s=s):
                    name, xb, c0, np_, xk = blk
                    b = bank()
                    tm_mm(blk, s, b)
                    headnorm(ps[b][0:np_, :], np_, 8, qnw, "qnw", qn2[bi % 2][0:np_, :], [("qn", bi % 2)], [PK(b)], 1)

                def post(bi=bi, blk=blk, half=half):
                    name, xb, c0, np_, xk = blk
                    qn_ = qn2[bi % 2]
                    b2 = bank()
                    pb = ps[b2][:].bitcast(BF16)
                    for c in range(4):
                        tr(pb[:, c * np_:(c + 1) * np_], qn_[0:np_, c * 128:(c + 1) * 128], identb[0:np_, 0:np_], [("qn", bi % 2), "identb"], [PK(b2)])
                    cp(qT[:, half * 4:half * 4 + 4, c0:c0 + np_], pb[:, 0:4 * np_].rearrange("p (g k) -> p g k", g=4), [PK(b2)],
                       [("qT", name, half)], eng="act")
                q_units.append((pre, post))
            pipe(q_units)

        ckpt(3)
        deferred_p_sc()
        for half in range(2):
            s = win(1536 + half * 512)

            def kparts(mc, s=s):
                return [(wsl[s][:, kc, mc * 128:(mc + 1) * 128], (lambda c0, c1, kc=kc: xnT[:, kc, c0:c1]),
                         (lambda tn, mc=mc, kc=kc: [("xn", kc), ("wsl", s, mc)])) for kc in range(16)]

            def evac(mc, tn, c0, c1, pa, pk, half=half):
                act(guT[:, half * 4 + mc, c0:c1], pa, AF.Gelu_apprx_tanh, [pk], [("guT", half * 4 + mc)])
            fm_group(4, kparts, evac, None)

        ckpt(4)
        s0 = win(2560); s1 = win(3072)
        S.add("dve", lambda e: e.memset(dummy[:], 0.0), [], [("xTh", c_) for c_ in range(16)] + [("gvf", 0), "gvo"])
        gv_units = []
        for ui, blk in enumerate([HB[1]] + MB + [SBk]):
            name = blk[0]
            if name == "h1":
                gb, gk = gvnh, ("gvnh",)
            elif name == "s":
                gb, gk = gvns, ("gvns",)
            else:
                gb, gk = gvn[ui % 2], ("gvn", ui % 2)

            def pre(blk=blk, gb=gb, gk=gk):
                name, xb, c0, np_, xk = blk
                b0 = bank(); b1 = bank()
                tm_mm(blk, s0, b0); tm_mm(blk, s1, b1)
                gf = gvf[0]
                FK = ("gvf", 0)
                act(gf[0:np_, 0:512], ps[b0][0:np_, :], AF.Gelu_apprx_tanh, [PK(b0)], [FK])
                act(gf[0:np_, 512:1024], ps[b1][0:np_, :], AF.Gelu_apprx_tanh, [PK(b1)], [FK])
                S.add("dve", lambda e, gf=gf, np_=np_: e.tensor_reduce(out=sstat[0:np_, 32:33], in_=gf[0:np_, :], axis=AX.X, op=ALU.add), [FK], ["sstat2"])
                ts(sstat[0:np_, 33:34], sstat[0:np_, 32:33], -1.0 / 1024.0, None, ALU.mult, None, ["sstat2"], ["sstat3"])
                act(gvo[0:np_, :], gf[0:np_, :], AF.Square, [FK, "sstat3"], ["gvo", "sstat4"], bias=sstat[0:np_, 33:34], scale=1.0,
                    accum=sstat[0:np_, 34:35])
                act(sstat[0:np_, 35:36], sstat[0:np_, 34:35], AF.Sqrt, ["sstat4", "epsb"], ["sstat5"], bias=epsb[0:np_, :], scale=1.0 / 1024.0)
                recip(sstat[0:np_, 35:36], "sstat5")
                tt(sstat[0:np_, 36:37], sstat[0:np_, 33:34], sstat[0:np_, 35:36], ALU.mult, ["sstat3", "sstat5"], ["sstat6"])
                act(gf[0:np_, :], gf[0:np_, :], AF.Identity, [FK, "sstat5", "sstat6"], [FK], bias=sstat[0:np_, 36:37], scale=sstat[0:np_, 35:36])
                tt(gb[0:np_, :], gf[0:np_, :], snw[0:np_, :], ALU.mult, [FK, "snw"], [gk])
                if name == "m3" or name == "s":
                    tt(gvo[0:np_, :], gf[0:np_, :], snw[0:np_, :], ALU.mult, [FK, "snw"], ["gvo"])
                    if name == "m3":
                        dma("sp", go[j], gvo[:], ["gvo"], [], "oG")
                    else:
                        dma("sp", gso[j], gvo[0:2, :], ["gvo"], [], "oG")

            def post(blk=blk, gb=gb, gk=gk):
                name = blk[0]
                if not name.startswith("m"):
                    return
                bi = int(name[1])
                for hh in range(2):
                    b = bank()
                    for c4 in range(4):
                        cc = hh * 4 + c4
                        g = cc // 2
                        o = ps[b][:, c4 * 128:(c4 + 1) * 128]
                        mm(o, gb[:, cc * 128:(cc + 1) * 128], wsT[:, g, :], True, False, [gk] + WST, [PK(b)])
                        mm(o, ones1[0:1, :], bh[0:1, g * 128:(g + 1) * 128], False, False, ["ones1", "bh"], [PK(b)])
                        mm(o, ones1[0:1, :], bl[0:1, g * 128:(g + 1) * 128], False, True, ["ones1", "bl"], [PK(b)])
                    gk4 = [("guT", hh * 4 + c) for c in range(4)]
                    tt(guT[:, hh * 4:hh * 4 + 4, bi * 128:(bi + 1) * 128], ps[b][:].rearrange("p (c t) -> p c t", c=4),
                       guT[:, hh * 4:hh * 4 + 4, bi * 128:(bi + 1) * 128], ALU.mult, [PK(b)] + gk4, gk4)
            gv_units.append((pre, post))
        pipe(gv_units)
        b = bank()
        for cc in range(8):
            g = cc // 2
            o = ps[b][:, cc * 4:cc * 4 + 2]
            mm(o, gvns[0:2, cc * 128:(cc + 1) * 128], w00I[0:2, g, :], True, False, [("gvns",), ("w00I", g)], [PK(b)])
            mm(o, ones1[0:1, :], bh[0:1, g * 128:g * 128 + 1].broadcast_to([1, 2]), False, False, ["ones1", "bh"], [PK(b)])
            mm(o, ones1[0:1, :], bl[0:1, g * 128:g * 128 + 1].broadcast_to([1, 2]), False, True, ["ones1", "bl"], [PK(b)])
            o = ps[b][:, cc * 4 + 2:cc * 4 + 4]
            mm(o, gvnh[:, cc * 128:(cc + 1) * 128], wsT[:, g, 126:128], True, False, [("gvnh",)] + WST, [PK(b)])
            mm(o, ones1[0:1, :], bh[0:1, g * 128 + 126:g * 128 + 128], False, False, ["ones1", "bh"], [PK(b)])
            mm(o, ones1[0:1, :], bl[0:1, g * 128 + 126:g * 128 + 128], False, True, ["ones1", "bl"], [PK(b)])
        gk8 = [("guT", c) for c in range(8)]
        tt(guT[:, :, 512:516], ps[b][:, 0:32].rearrange("p (c t) -> p c t", c=8), guT[:, :, 512:516], ALU.mult, [PK(b)] + gk8, gk8)

        ckpt(5)
        def attn_norm(bn, bd, g, nq, qc0, tn):
            rc = scr[2]
            S.add("dve", lambda e: e.reciprocal(out=rc[:, 0:4 * nq], in_=ps[bd][:, 0:4 * nq]), [PK(bd)], [("scr", 2)])
            for par in range(2):
                lo, hi = par * 64, par * 64 + 64
                tt(attnT[lo:hi, 2 * g:2 * g + 2, qc0:qc0 + nq], ps[bn][lo:hi, par * 2 * nq:(par + 1) * 2 * nq].rearrange("p (b q) -> p b q", b=2),
                   rc[lo:hi, par * 2 * nq:(par + 1) * 2 * nq].rearrange("p (b q) -> p b q", b=2), ALU.mult, [PK(bn), ("scr", 2)],
                   [("attnT", g, par)])

        at_units = []
        for c in range(4):
            qc0 = c * 128
            if c == 0:
                kA, vA, mA, rdA = kTh[:, :, 128:256], vduph[:, 1, :], maskab, [("kTh",), ("vduph",), "maskab"]
            else:
                kA, vA, mA, rdA = kT[:, :, qc0 - 128:qc0], vdup[:, c - 1, :], maskab, [("kT", "m%d" % (c - 1)), ("vdup", c - 1), "maskab"]
            kB, vB = kT[:, :, qc0:qc0 + 128], vdup[:, c, :]
            rdB = [("kT", "m%d" % c), ("vdup", c)]
            for g in range(4):
                ui = len(at_units)

                def pre(c=c, g=g, qc0=qc0, kA=kA, kB=kB, mA=mA, rdA=rdA, rdB=rdB, ui=ui):
                    pbs = pbuf2[ui % 3]
                    bP = [bank(), bank()]
                    for par in range(2):
                        lo, hi = par * 64, par * 64 + 64
                        for r, kk in enumerate((kA, kB)):
                            mm(ps[bP[par]][:, r * 256:(r + 1) * 256], kk[lo:hi, g, :], qT[lo:hi, 2 * g:2 * g + 2, qc0:qc0 + 128], True, True,
                               rdA + rdB + [("qT", "m%d" % c, g // 2)], [PK(bP[par])])
                        if c == 0:
                            act(ebuf[par][:, 0:256], ps[bP[par]][:, 0:256], AF.Exp, [PK(bP[par]), "hvb"], [("ebuf", par)], scale=0.125, bias=hvb[:])
                            act(ebuf[par][:, 256:512], ps[bP[par]][:, 256:512], AF.Exp, [PK(bP[par])], [("ebuf", par)], scale=0.125)
                        else:
                            act(ebuf[par][:], ps[bP[par]][:], AF.Exp, [PK(bP[par])], [("ebuf", par)], scale=0.125)
                        tt(pbs[par][:].rearrange("p (r b q) -> p r b q", r=2, b=2), ebuf[par][:].rearrange("p (r b q) -> p r b q", r=2, b=2),
                           mA[:, :, 4 * g + par:4 * g + par + 3:2, :], ALU.mult, [("ebuf", par)] + rdA, [("pbuf", ui % 3, par)])

                def post(c=c, g=g, qc0=qc0, vA=vA, vB=vB, rdA=rdA, rdB=rdB, ui=ui):
                    pbs = pbuf2[ui % 3]
                    bn, bd = bank(), bank()
                    for (bo, use_v) in ((bn, True), (bd, False)):
                        for par in range(2):
                            for r, vv in enumerate((vA, vB)):
                                lh = vv[:, g * 128:(g + 1) * 128] if use_v else ones1[:]
                                mm(ps[bo][:, par * 256:(par + 1) * 256], lh, pbs[par][:, r * 256:(r + 1) * 256], r == 0, (r == 1) and use_v,
                                   rdA + rdB + [("pbuf", ui % 3, par), "ones1"], [PK(bo)])
                            if not use_v:
                                for si_, (sx_, sk_) in enumerate(((sxh, "sxh"), (sxl, "sxl"))):
                                    mm(ps[bo][:, par * 256:(par + 1) * 256], ones1[0:1, :],
                                       sx_[0:1, g, par, :].unsqueeze(2).broadcast_to([1, 2, 128]), False, si_ == 1, ["ones1", sk_], [PK(bo)])
                    attn_norm(bn, bd, g, 128, qc0, "m")
                at_units.append((pre, post))
        pipe(at_units, depth=2)
        ckpt(6)
        kbs = [(kTs[:, 0, :, :], vdups[:, 0, :], 128, [("kTs",), ("vdups",)]),
               (kTs[:, 1, :, :], vdups[:, 1, :], 128, [("kTs",), ("vdups",)]),
               (kT[:, :, 512:514], vdup[:, 4, :], 2, [("kT", "s"), ("vdup", 4)]),
               (kTh[:, :, 0:128], vduph[:, 0, :], 128, [("kTh",), ("vduph",)]),
               (kTh[:, :, 128:256], vduph[:, 1, :], 128, [("kTh",), ("vduph",)])]
        for g in range(4):
            bPs = [bank(), bank()]
            pPs = [ps[bb_][:, 0:40].rearrange("p (k b q) -> p k b q", k=5, b=2) for bb_ in bPs]
            for par in range(2):
                lo, hi = par * 64, par * 64 + 64
                for ki, (kk, vv, nk, rd) in enumerate(kbs):
                    mm(pPs[par][0:nk, ki, :, :], kk[lo:hi, g, :], qT[lo:hi, 2 * g:2 * g + 2, 512:516], True, True,
                       rd + [("qT", "s", g // 2)], [PK(bPs[par])])
            eb = ebuf[0][:, 0:80].rearrange("p (a k b q) -> p a k b q", a=2, k=5, b=2)
            pb_ = pbuf[0][:, 0:80].rearrange("p (a k b q) -> p a k b q", a=2, k=5, b=2)
            for par in range(2):
                for (k0, k1, np2) in ((0, 2, 128), (3, 5, 128), (2, 3, 2)):
                    act(eb[0:np2, par, k0:k1], pPs[par][0:np2, k0:k1], AF.Exp, [PK(bPs[par])], [("ebuf", 0)], scale=0.125)
                    tt(pb_[0:np2, par, k0:k1], eb[0:np2, par, k0:k1], masks[0:np2, k0:k1, 4 * g + par:4 * g + par + 3:2, :], ALU.mult,
                       [("ebuf", 0), "masks"], [("pbuf", 0, 0)])
            bn, bd = bank(), bank()
            for (bo, use_v) in ((bn, True), (bd, False)):
                for par in range(2):
                    for ki, (kk, vv, nk, rd) in enumerate(kbs):
                        lh = vv[0:nk, g * 128:(g + 1) * 128] if use_v else ones1[0:nk, :]
                        mm(ps[bo][:, par * 8:(par + 1) * 8], lh, pb_[0:nk, par, ki, :, :], ki == 0, (ki == 4) and use_v, rd + [("pbuf", 0, 0), "ones1"], [PK(bo)])
                    if not use_v:
                        for si_, (sx_, sk_) in enumerate(((sxh, "sxh"), (sxl, "sxl"))):
                            mm(ps[bo][:, par * 8:(par + 1) * 8], ones1[0:1, :],
                               sx_[0:1, g, par, :].unsqueeze(2).broadcast_to([1, 2, 4]), False, si_ == 1, ["ones1", sk_], [PK(bo)])
            attn_norm(bn, bd, g, 4, 512, "s")

        ckpt(7)
        S.add("dve", lambda e: e.memset(dummy[:], 0.0), [], ["ARENA"])
        AK = lambda tn: [("attnT", g_, p_) for g_ in range(4) for p_ in range(2)]
        GK = lambda tn: [("guT", c_) for c_ in range(8)]
        for cg in range(4):
            for which in range(2):
                gc0 = which * D + cg * 512
                s = wload([((lambda w, h_=h_: w[:, :, h_ * 256:(h_ + 1) * 256]),
                            w_gate[:, gc0 + h_ * 256:gc0 + (h_ + 1) * 256].rearrange("(kc p) n -> p kc n", p=128),
                            (h_ * 256, (h_ + 1) * 256)) for h_ in range(2)])

                def kparts(mc, s=s):
                    return [(wsl[s][:, kc, mc * 128:(mc + 1) * 128], (lambda c0, c1, kc=kc: xnT[:, kc, c0:c1]),
                             (lambda tn, mc=mc, kc=kc: [("xn", kc), ("wsl", s, mc)])) for kc in range(16)]

                def evac(mc, tn, c0, c1, pa, pk, which=which, cg=cg):
                    ch = which * 16 + cg * 4 + mc
                    act(gsb[which][:, mc, c0:c1], pa, AF.Sigmoid, [pk, "bgT"], [("gsb", which, mc)], bias=bgT[:, ch:ch + 1], scale=1.0)
                fm_group(4, kparts, evac, None)
            brp = []
            for h_ in range(2):
                brp.append(((lambda w, h_=h_: w[:, 0:8, h_ * 256:(h_ + 1) * 256]),
                            w_bra[:, cg * 512 + h_ * 256:cg * 512 + (h_ + 1) * 256].rearrange("(kc p) n -> p kc n", p=128), (h_ * 256, (h_ + 1) * 256)))
                brp.append(((lambda w, h_=h_: w[:, 8:16, h_ * 256:(h_ + 1) * 256]),
                            w_brg[:, cg * 512 + h_ * 256:cg * 512 + (h_ + 1) * 256].rearrange("(kc p) n -> p kc n", p=128), (h_ * 256, (h_ + 1) * 256)))
            s = wload(brp)
            for mc in range(4):
                for (tn, c0, c1) in TILES:
                    ba, bm = bank(), bank()
                    n = c1 - c0
                    for kc in range(8):
                        mm(ps[ba][:, 0:n], wsl[s][:, kc, mc * 128:(mc + 1) * 128], attnT[:, kc, c0:c1], kc == 0, kc == 7, AK(tn) + [("wsl", s, mc)], [PK(ba)])
                    for kc in range(8):
                        mm(ps[bm][:, 0:n], wsl[s][:, 8 + kc, mc * 128:(mc + 1) * 128], guT[:, kc, c0:c1], kc == 0, kc == 7, GK(tn) + [("wsl", s, mc)], [PK(bm)])
                    tt(scr[3][:, 0:n], ps[ba][:, 0:n], gsb[1][:, mc, c0:c1], ALU.mult, [PK(ba), ("gsb", 1, mc)], [("scr", 1)])
                    tt(scr[0][:, 0:n], ps[bm][:, 0:n], gsb[0][:, mc, c0:c1], ALU.mult, [PK(bm), ("gsb", 0, mc)], [("scr", 0)])
                    tt(mergedT[:, cg * 4 + mc, c0:c1], scr[3][:, 0:n], scr[0][:, 0:n], ALU.add, [("scr", 1), ("scr", 0)], [("mg", cg * 4 + mc)])
        MK = lambda tn: [("mg", c_) for c_ in range(16)]

        def resid_evac(cgv):
            def evac(mc, tn, c0, c1, pa, pk):
                ch = cgv * 4 + mc
                tt(xT[:, ch, c0:c1], pa, xT[:, ch, c0:c1], ALU.add, [pk, ("xT", ch)], [("xT", ch)])
            return evac

        for cg in range(4):
            s = wload([((lambda w, h_=h_: w[:, :, h_ * 256:(h_ + 1) * 256]),
                        w_out[:, cg * 512 + h_ * 256:cg * 512 + (h_ + 1) * 256].rearrange("(kc p) n -> p kc n", p=128),
                        (h_ * 256, (h_ + 1) * 256)) for h_ in range(2)])

            def kparts(mc, s=s):
                return [(wsl[s][:, kc, mc * 128:(mc + 1) * 128], (lambda c0, c1, kc=kc: mergedT[:, kc, c0:c1]),
                         (lambda tn, mc=mc: MK(tn) + [("wsl", s, mc)])) for kc in range(16)]
            fm_group(4, kparts, resid_evac(cg), None)

        ckpt(8)
        while pending:
            step()
        for s_ in range(2):
            for (src_t, dst_t) in ((sk, kso), (sv, vso)):
                dma("sp", dst_t[j, s_, 0:112, :], src_t[j, s_, 1:113, :], [], [], "o1")
                dma("sp", dst_t[j, s_, 112:127, :], src_t[j, s_, 113:128, :], [], [], "o1")
            dma("sp", cso[j, s_, 0:1, :], sc[j, 2 * s_ + 1:2 * s_ + 2, :], [], [], "o1")
        dma("pool", wppb[:], w_pp.rearrange("(kc p) n -> p kc n", p=128), [], ["wppb"], "cm2")
        for (_tn, _c0, _c1) in TILES:
            norm_fm(xT, XM, (_c0, _c1), 1, xnT, [("xn",)])

        def up_group(G):
            ab = actb[G % 2]
            AKEY = ("actb", G % 2)
            jl = list(range(G * 11, G * 11 + 11))
            for f0 in range(0, 11, 2):
                pairs = jl[f0:f0 + 2]
                npair = len(pairs)
                j0 = pairs[0]
                s = wload([(lambda w, npair=npair: w[:, :, 0:npair * 128], w_up[:, j0 * 128:(j0 + npair) * 128].rearrange("(kc p) n -> p kc n", p=128), (0, npair * 128)),
                           (lambda w, npair=npair: w[:, :, 256:256 + npair * 128], w_up[:, DFF + j0 * 128:DFF + (j0 + npair) * 128].rearrange("(kc p) n -> p kc n", p=128), (256, 256 + npair * 128))])
                for pi, jj in enumerate(pairs):
                    jloc = jj - G * 11
                    bks = []
                    for part in range(2):
                        wc0 = part * 256 + pi * 128
                        bm_, bs_ = bank(), bank()
                        bks.append((bm_, bs_))
                        for (bb, c0, c1, tn) in ((bm_, 0, 258, "a"), (bs_, 258, 516, "b")):
                            for kc in range(16):
                                mm(ps[bb][:, 0:c1 - c0], wsl[s][:, kc, wc0:wc0 + 128], xnT[:, kc, c0:c1], kc == 0, kc == 15,
                                   [("xn", kc), ("wsl", s, part * 2 + pi)], [PK(bb)])
                    for part in range(2):
                        ch = part * 44 + jj
                        bm_, bs_ = bks[part]
                        hb_, cb_ = hbuf[part], cbuf[part]
                        sc2, bi2 = cwT[:, 2, ch:ch + 1], cbT[:, ch:ch + 1]
                        act(hb_[:, 2:260], ps[bm_][:, 0:258], AF.Copy, [PK(bm_)], [("hbuf", part, 0)])
                        act(cb_[:, 0:258], ps[bm_][:, 0:258], AF.Identity, [PK(bm_), "cwT", "cbT"], [("cbuf", part, 0)], bias=bi2, scale=sc2)
                        act(hb_[:, 260:514], ps[bs_][:, 0:254], AF.Copy, [PK(bs_)], [("hbuf", part, 1)])
                        act(cb_[:, 258:512], ps[bs_][:, 0:254], AF.Identity, [PK(bs_), "cwT", "cbT"], [("cbuf", part, 1)], bias=bi2, scale=sc2)
                        act(hb_[:, 0:2], ps[bs_][:, 256:258], AF.Copy, [PK(bs_)], [("hbuf", part, 2)])
                        act(hS[:, ch, :], ps[bs_][:, 254:256], AF.Copy, [PK(bs_)], [("hS", ch)])
                        act(cS[:, part, :], ps[bs_][:, 254:256], AF.Identity, [PK(bs_), "cwT", "cbT"], [("cS", part)], bias=bi2, scale=sc2)
                    HK = lambda p_: [("hbuf", p_, 0), ("hbuf", p_, 1), ("hbuf", p_, 2)]
                    CK = lambda p_: [("cbuf", p_, 0), ("cbuf", p_, 1)]
                    for tap, (o0, o1) in ((1, (1, 513)), (0, (0, 512))):
                        for part in range(2):
                            ch = part * 44 + jj
                            stt(cbuf[part][:], hbuf[part][:, o0:o1], cwT[:, tap, ch:ch + 1], cbuf[part][:], ALU.mult, ALU.add,
                                HK(part) + ["cwT"] + CK(part), CK(part))
                    for part in range(2):
                        ch = part * 44 + jj
                        cp(hlast[:, ch, :], hbuf[part][:, 512:514], HK(part), [("hlast", ch)])
                    for tap, sl2 in ((1, slice(1, 4, 2)), (0, slice(0, 3, 2))):
                        for part in range(2):
                            ch = part * 44 + jj
                            stt(cS[:, part, :], stT[:, ch, sl2], cwT[:, tap, ch:ch + 1], cS[:, part, :], ALU.mult, ALU.add,
                                [("stT",), "cwT", ("cS", part)], [("cS", part)])
                    act(sgb[:, 0:512], cbuf[0][:], AF.Silu, CK(0), ["sgb"])
                    act(sgb[:, 512:514], cS[:, 0, :], AF.Silu, [("cS", 0)], ["sgb2"])
                    tt(ab[:, jloc, 0:512], sgb[:, 0:512], cbuf[1][:], ALU.mult, ["sgb"] + CK(1), [AKEY])
                    tt(ab[:, jloc, 512:514], sgb[:, 512:514], cS[:, 1, :], ALU.mult, ["sgb2", ("cS", 1)], [(AKEY[0], AKEY[1], "s")])
                    S.add("dve", lambda e, ab=ab, jloc=jloc: e.memset(ab[:, jloc, 514:516], 0.0), [], [(AKEY[0], AKEY[1], "z")])

        def down_group(G):
            ab = actb[G % 2]
            AKEY = ("actb", G % 2)
            for cg in range(4):
                s = wload([((lambda w, h_=h_: w[:, 0:11, h_ * 256:(h_ + 1) * 256]),
                            w_down[G * 11 * 128:(G + 1) * 11 * 128, cg * 512 + h_ * 256:cg * 512 + (h_ + 1) * 256].rearrange("(kc p) n -> p kc n", p=128),
                            (h_ * 256, (h_ + 1) * 256)) for h_ in range(2)])

                def kparts(mc, s=s):
                    return [(wsl[s][:, jl_, mc * 128:(mc + 1) * 128], (lambda c0, c1, jl_=jl_: ab[:, jl_, c0:c1]),
                             (lambda tn: [AKEY, (AKEY[0], AKEY[1], "s"), (AKEY[0], AKEY[1], "z"), ("wsl", s, mc)])) for jl_ in range(11)]
                fm_group(4, kparts, resid_evac(cg), None)

        up_group(0); up_group(1); down_group(0); up_group(2); down_group(1); up_group(3); down_group(2); down_group(3)

        def fm2tm_small(srcbuf, skeys_fn, dst_fn, dsem):
            for pc in range(0, 88, 8):
                n_ = min(8, 88 - pc)
                sl_ = nslot()
                for q0 in range(0, n_, 4):
                    b = bank()
                    for i in range(4):
                        tr(ps[b][0:2, i * 128:(i + 1) * 128], srcbuf[:, pc + q0 + i, :], ident[:], [skeys_fn(pc + q0 + i), "ident"], [PK(b)])
                    cp(ytm[sl_][0:2, q0 * 128:(q0 + 4) * 128], ps[b][0:2, :], [PK(b)], [("xtm", sl_)])
                dma("sp", dst_fn(pc * 128, (pc + n_) * 128), ytm[sl_][0:2, 0:n_ * 128], [("xtm", sl_)], [], f"xin{sl_}")
        fm2tm_small(hlast, lambda ch: ("hlast", ch), lambda a, b_: co[j, :, a:b_], "o3")
        fm2tm_small(hS, lambda ch: ("hS", ch), lambda a, b_: cso[j, :, 1, a:b_], "o3")

        ckpt(9)
        for (_tn, _c0, _c1) in TILES:
            norm_fm(xT, XM, (_c0, _c1), 2, xnT, [("xn",)])
        for cg in range(4):
            s = wload([((lambda w, h_=h_: w[:, :, h_ * 256:(h_ + 1) * 256]),
                        w_pg[:, cg * 512 + h_ * 256:cg * 512 + (h_ + 1) * 256].rearrange("(kc p) n -> p kc n", p=128),
                        (h_ * 256, (h_ + 1) * 256)) for h_ in range(2)])
            for mc in range(4):
                ch = cg * 4 + mc
                for (tn, c0, c1) in TILES:
                    n = c1 - c0
                    bg_, bp_ = bank(), bank()
                    for kc in range(16):
                        mm(ps[bg_][:, 0:n], wsl[s][:, kc, mc * 128:(mc + 1) * 128], xnT[:, kc, c0:c1], kc == 0, kc == 15, [("xn", kc), ("wsl", s, mc)], [PK(bg_)])
                    for kc in range(2):
                        mm(ps[bp_][:, 0:n], wppb[:, kc, ch * 128:(ch + 1) * 128], pT[:, kc, c0:c1], kc == 0, kc == 1, [("pT",), "wppb"], [PK(bp_)])
                    act(scr[3][:, 0:n], ps[bg_][:, 0:n], AF.Sigmoid, [PK(bg_)], [("scr", 1)])
                    tt(scr[3][:, 0:n], scr[3][:, 0:n], ps[bp_][:, 0:n], ALU.mult, [("scr", 1), PK(bp_)], [("scr", 1)])
                    tt(xT[:, ch, c0:c1], scr[3][:, 0:n], xT[:, ch, c0:c1], ALU.add, [("scr", 1), ("xT", ch)], [("xT", ch)])

        ckpt(10)
        for blk in range(4):
            for hf in range(2):
                sl_ = nslot()
                for q2 in range(2):
                    q = hf * 2 + q2
                    b = bank()
                    for i in range(4):
                        tr(ps[b][:, i * 128:(i + 1) * 128], xT[:, q * 4 + i, blk * 128:(blk + 1) * 128], ident[:], [("xT", q * 4 + i), "ident"], [PK(b)])
                    cp(ytm[sl_][:, q2 * 512:(q2 + 1) * 512], ps[b][:], [PK(b)], [("xtm", sl_)], eng=("act" if q % 2 else "dve"))
                dma("sp", ym[j, blk * 128:(blk + 1) * 128, hf * 1024:(hf + 1) * 1024], ytm[sl_][:], [("xtm", sl_)], [], f"xin{sl_}")
        for hf in range(2):
            sl_ = nslot()
            for q2 in range(2):
                q = hf * 2 + q2
                b = bank()
                for i in range(4):
                    tr(ps[b][0:2, i * 128:(i + 1) * 128], xT[:, q * 4 + i, 512:514], ident[:], [("xT", q * 4 + i), "ident"], [PK(b)])
                cp(ytm[sl_][0:2, q2 * 512:(q2 + 1) * 512], ps[b][0:2, :], [PK(b)], [("xtm", sl_)])
            dma("sp", ys[j, :, hf * 1024:(hf + 1) * 1024], ytm[sl_][0:2, :], [("xtm", sl_)], [], f"xin{sl_}")

      except _Stop:
        pass
    S.emit(nc)
    es.close()
    return nc, S


def _consts():
    h = np.arange(1, NH + 1, dtype=np.float64)
    sl = np.exp2(-8.0 * h / NH)
    jj = np.arange(128)[:, None, None].astype(np.float64)
    ii = np.arange(128)[None, None, :].astype(np.float64)
    s3 = sl[None, :, None]
    mc = np.where(ii >= jj, np.exp(-s3 * (ii - jj)), 0.0)
    mp = np.where(ii <= jj, np.exp(-s3 * (128.0 + ii - jj)), 0.0)
    maskab = np.stack([mp, mc], axis=1).astype(np.float32)
    mask0 = maskab.copy()
    mask0[:, 0] = 0.0
    ms = np.zeros((128, 5, NH, 4), np.float32)
    ms[:, 0, :, 0] = mp[:, :, 0]
    ms[:, 1, :, 1] = mp[:, :, 0]
    ms[0, 2, :, 0] = 1.0
    ms[1, 2, :, 1] = 1.0
    ms[:, 3, :, 2] = mp[:, :, 126]; ms[:, 3, :, 3] = mp[:, :, 127]
    ms[:, 4, :, 2] = mc[:, :, 126]; ms[:, 4, :, 3] = mc[:, :, 127]
    tri = (np.arange(128)[:, None] <= np.arange(128)[None, :]).astype(np.float32)
    return dict(maskab=maskab.reshape(128, -1), mask0=mask0.reshape(128, -1), masks=ms.reshape(128, -1),
                ident=np.eye(128, dtype=np.float32), tri=tri)


_PROG = {}


def make_core_inputs(inp, jobs, C):
    f = np.float32
    nj = len(jobs)
    xm = np.zeros((nj, 512, D), f); xh = np.zeros((nj, 256, D), f); xs = np.zeros((nj, 4, D), f)
    pm = np.zeros((nj, 516, 256), f); sk = np.zeros((nj, 2, 128, 256), f); sv = np.zeros((nj, 2, 128, 256), f)
    sc = np.zeros((nj, 4, 2 * DFF), f); hvb = np.zeros((nj, 128, 1), f)
    for i, (b, qd, ss) in enumerate(jobs):
        t0 = qd * 512
        xm[i] = inp["x_prompt"][b, t0:t0 + 512]
        pm[i, 0:512] = inp["p_prompt"][0, b, t0:t0 + 512]
        if qd > 0:
            xh[i] = inp["x_prompt"][b, t0 - 256:t0]
            xs[i, 2:4] = inp["x_prompt"][b, t0 - 2:t0]
        else:
            hvb[i] = -30000.0
        for k, s_ in enumerate(ss):
            xs[i, k] = inp["x_sample"][s_, 0]
            pm[i, 512 + k] = inp["p_sample"][0, s_, 0]
            sk[i, k] = inp["state_attn_k"][0, s_].reshape(128, 256)
            sv[i, k] = inp["state_attn_v"][0, s_].reshape(128, 256)
            sc[i, 2 * k:2 * k + 2] = inp["state_conv"][0, s_]
    m = dict(xm=xm, xh=xh, xs=xs, pm=pm, sk=sk, sv=sv, sc=sc, hvb=hvb,
             maskab=C["maskab"], masks=C["masks"], ident=C["ident"], tri=C["tri"])
    for k in ("w_in", "w_br_attn", "w_br_gm", "w_gate", "w_out", "w_up", "w_down", "w_ple_gate", "w_ple_proj", "sgu_w"):
        m[k] = np.ascontiguousarray(inp[k][0], dtype=f)
    for k in ("attn_norm_w", "ffn_norm_w", "ple_norm_w", "q_norm_w", "k_norm_w", "attn_sinks", "sgu_norm_w", "b_gate", "conv_b"):
        m[k] = np.ascontiguousarray(inp[k][0], dtype=f)
    m["sgu_b"] = np.ascontiguousarray(inp["sgu_b"][0].reshape(512), dtype=f)
    m["conv_w"] = np.ascontiguousarray(inp["conv_w"][0], dtype=f)
    return m


def kernel(**inputs):
    inp = {k: np.asarray(v) for k, v in inputs.items()}
    if "nc" not in _PROG:
        _PROG["nc"] = build_program(NJ)[0]
    nc = _PROG["nc"]
    C = _consts()
    core_jobs = []
    for c in range(8):
        jobs = []
        for jl in range(NJ):
            jg = c * NJ + jl
            jobs.append((jg // 4, jg % 4, [2 * jg, 2 * jg + 1]))
        core_jobs.append(jobs)
    in_maps = [make_core_inputs(inp, jobs, C) for jobs in core_jobs]
    res = run_bass_kernel_spmd(nc, in_maps, core_ids=list(range(8)))
    f = np.float32
    y_p = np.zeros((4, 2048, D), f); y_s = np.zeros((32, 1, D), f)
    akp = np.zeros((1, 4, 128, 4, 64), f); avp = np.zeros((1, 4, 128, 4, 64), f)
    aks = np.zeros((1, 32, 128, 4, 64), f); avs = np.zeros((1, 32, 128, 4, 64), f)
    gp = np.zeros((1, 4, 128, 1024), f); gs = np.zeros((1, 32, 1, 1024), f)
    cpo = np.zeros((1, 4, 2, 2 * DFF), f); cs = np.zeros((1, 32, 2, 2 * DFF), f)
    for c in range(8):
        r = res.results[c]
        for i, (b, qd, ss) in enumerate(core_jobs[c]):
            y_p[b, qd * 512:(qd + 1) * 512] = r["ym"][i]
            for k, s_ in enumerate(ss):
                y_s[s_, 0] = r["ys"][i, k]
                aks[0, s_] = r["kso"][i, k].reshape(128, 4, 64); avs[0, s_] = r["vso"][i, k].reshape(128, 4, 64)
                gs[0, s_, 0] = r["gso"][i, k]
                cs[0, s_] = r["cso"][i, k]
            if qd == 3:
                akp[0, b] = r["ko"][i].reshape(128, 4, 64); avp[0, b] = r["vo"][i].reshape(128, 4, 64)
                gp[0, b] = r["go"][i]
                cpo[0, b] = r["co"][i]
    return (y_p, y_s, akp, avp, aks, avs, gp, gs, cpo, cs)
```

```python
import contextlib
import numpy as np
import concourse.bass as bass
import concourse.mybir as mybir
from concourse.bass_utils import run_bass_kernel_spmd

F32 = mybir.dt.float32
BF16 = mybir.dt.bfloat16
AF = mybir.ActivationFunctionType
ALU = mybir.AluOpType
AX = mybir.AxisListType

D = 2048
NJ = 2
NMAIN = 512
CM = 516
DFF = 5632
NH = 16
EPS = 1e-6
ENGS = ("pe", "act", "dve", "pool", "sp")
ARENA_KEYS = {"xTh", "xnh", "qT", "kT", "kTh", "kTs", "vdup", "vduph", "vdups", "gvn", "gvnh", "gvns", "gvf", "gvo",
              "kdup", "qn", "knf", "vf", "ebuf", "pbuf", "mg", "gsb", "actb", "hbuf", "cbuf", "sgb", "sgb2", "wppb"}


class Op:
    __slots__ = ("eng", "fn", "deps", "idx", "gidx", "signal", "sigval", "dsem", "dval", "clock")

    def __init__(self, eng, fn):
        self.eng = eng
        self.fn = fn
        self.deps = []
        self.signal = False
        self.sigval = 0
        self.dsem = None
        self.dval = 0
        self.clock = None


class Sched:
    def __init__(self):
        self.ops = {e: [] for e in ENGS}
        self.all = []
        self.last_w = {}
        self.readers = {}
        self.dma_cnt = {}

    maxops = 10 ** 9

    def add(self, eng, fn, reads=(), writes=(), dsem=None):
        if len(self.all) >= self.maxops:
            return None
        reads = list(reads)
        exp_ = []
        for k in reads:
            if isinstance(k, tuple) and len(k) == 2 and k[0] == "wsl":
                exp_ += [("wsl", k[1], q_) for q_ in range(4)]
            else:
                exp_.append(k)
        reads = exp_
        for k in reads + list(writes):
            kn_ = k[0] if isinstance(k, tuple) else k
            if kn_ in ARENA_KEYS:
                reads.append("ARENA")
                break
        op = Op(eng, fn)
        deps = {}
        for k in list(reads) + list(writes):
            w = self.last_w.get(k)
            if w is not None:
                deps[id(w)] = w
        for k in writes:
            for r in self.readers.get(k, ()):
                deps[id(r)] = r
        for k in reads:
            if isinstance(k, tuple) and k[0] == "ps":
                for r in self.readers.get(k, ()):
                    if r.eng != eng:
                        deps[id(r)] = r
        for d in deps.values():
            if d.dsem is None and d.eng == eng and eng == "pe":
                continue
            op.deps.append(d)
        for k in writes:
            self.last_w[k] = op
            self.readers[k] = []
        for k in reads:
            self.readers.setdefault(k, []).append(op)
        if dsem is not None:
            op.dsem = dsem
            self.dma_cnt[dsem] = self.dma_cnt.get(dsem, 0) + 16
            op.dval = self.dma_cnt[dsem]
        op.idx = len(self.ops[eng])
        op.gidx = len(self.all)
        self.ops[eng].append(op)
        self.all.append(op)
        return op

    def finalize(self):
        for op in self.all:
            for d in op.deps:
                if d.dsem is None:
                    d.signal = True
        for e in ENGS:
            c = 0
            for op in self.ops[e]:
                if op.dsem is None and op.signal:
                    c += 1
                    op.sigval = c
        known = {e: {} for e in ENGS}
        self.waits = {}
        for op in self.all:
            kn = known[op.eng]
            wm = {}
            for d in sorted(op.deps, key=lambda d: -d.gidx):
                if d.dsem is not None:
                    key, val = ("D", d.dsem), d.dval
                else:
                    key, val = ("E", d.eng), d.sigval
                if kn.get(key, 0) >= val:
                    continue
                wm[key] = max(wm.get(key, 0), val)
                for k2, v2 in d.clock.items():
                    if kn.get(k2, 0) < v2:
                        kn[k2] = v2
                kn[key] = max(kn.get(key, 0), val)
            self.waits[op.gidx] = list(wm.items())
            clk = dict(kn)
            if op.dsem is not None:
                clk[("D", op.dsem)] = max(clk.get(("D", op.dsem), 0), op.dval)
            elif op.signal:
                clk[("E", op.eng)] = max(clk.get(("E", op.eng), 0), op.sigval)
            op.clock = clk

    def emit(self, nc):
        self.finalize()
        dsems = sorted(self.dma_cnt.keys())
        with contextlib.ExitStack() as es:
            sems = {}
            for e in ENGS:
                sems[("E", e)] = es.enter_context(nc.semaphore("s_" + e))
            for d in dsems:
                sems[("D", d)] = es.enter_context(nc.semaphore("d_" + d))
            block = es.enter_context(nc.Block())

            def run(engname, eng):
                for op in self.ops[engname]:
                    for key, val in self.waits[op.gidx]:
                        eng.wait_ge(sems[key], val)
                    ins = op.fn(eng)
                    if op.dsem is not None:
                        ins.then_inc(sems[("D", op.dsem)], 16)
                    elif op.signal:
                        ins.then_inc(sems[("E", engname)], 1)
                if engname == "sp":
                    for d in dsems:
                        eng.wait_ge(sems[("D", d)], self.dma_cnt[d])
                    for e2 in ENGS:
                        tot = max([o.sigval for o in self.ops[e2] if o.dsem is None] + [0])
                        if tot:
                            eng.wait_ge(sems[("E", e2)], tot)

            block.tensor(lambda pe: run("pe", pe))
            block.scalar(lambda a: run("act", a))
            block.vector(lambda v: run("dve", v))
            block.gpsimd(lambda g: run("pool", g))
            block.sync(lambda s: run("sp", s))


class _Stop(Exception):
    pass


def build_program(nj=NJ, stop=99):
    nc = bass.Bass("TRN2", target_bir_lowering=False)

    def ckpt(n):
        if n >= stop:
            raise _Stop()

    def din(name, shape):
        return nc.dram_tensor(name, list(shape), F32, kind="ExternalInput").ap()

    def dout(name, shape):
        return nc.dram_tensor(name, list(shape), F32, kind="ExternalOutput").ap()

    xm = din("xm", [nj, 512, D]); xh = din("xh", [nj, 256, D]); xs = din("xs", [nj, 4, D])
    pm = din("pm", [nj, 516, 256])
    sk = din("sk", [nj, 2, 128, 256]); sv = din("sv", [nj, 2, 128, 256])
    sc = din("sc", [nj, 4, 2 * DFF])
    hvb_d = din("hvb", [nj, 128, 1])
    maskab_d = din("maskab", [128, 2 * NH * 128]); masks_d = din("masks", [128, 5 * NH * 4])
    ident_d = din("ident", [128, 128]); tri_d = din("tri", [128, 128])
    w_in = din("w_in", [D, 3584]); w_bra = din("w_br_attn", [1024, D]); w_brg = din("w_br_gm", [1024, D])
    w_gate = din("w_gate", [D, 2 * D]); w_out = din("w_out", [D, D]); w_up = din("w_up", [D, 2 * DFF])
    w_down = din("w_down", [DFF, D]); w_pg = din("w_ple_gate", [D, D]); w_pp = din("w_ple_proj", [256, D])
    nw1_d = din("attn_norm_w", [D]); nw2_d = din("ffn_norm_w", [D]); nw3_d = din("ple_norm_w", [D])
    qnw_d = din("q_norm_w", [64]); knw_d = din("k_norm_w", [64]); sink_d = din("attn_sinks", [NH])
    snw_d = din("sgu_norm_w", [1024]); sguw_d = din("sgu_w", [4, 128, 128]); sgub_d = din("sgu_b", [512])
    bg_d = din("b_gate", [2 * D]); cw_d = din("conv_w", [3, 2 * DFF]); cb_d = din("conv_b", [2 * DFF])
    ym = dout("ym", [nj, 512, D]); ys = dout("ys", [nj, 2, D])
    ko = dout("ko", [nj, 128, 256]); vo = dout("vo", [nj, 128, 256])
    kso = dout("kso", [nj, 2, 128, 256]); vso = dout("vso", [nj, 2, 128, 256])
    go = dout("go", [nj, 128, 1024]); gso = dout("gso", [nj, 2, 1024])
    co = dout("co", [nj, 2, 2 * DFF]); cso = dout("cso", [nj, 2, 2, 2 * DFF])

    S = Sched()
    import os as _os
    S.maxops = int(_os.environ.get("MAXOPS", 10 ** 9))
    es = contextlib.ExitStack()

    def sb(name, shape, dt=F32):
        return es.enter_context(nc.sbuf_tensor("sb_" + name, list(shape), dt))

    xT = sb("xT", [128, 16, CM]); xnT = sb("xnT", [128, 16, CM], BF16)
    pT = sb("pT", [128, 2, CM], BF16)
    xtm = [sb(f"xtm{i}", [128, 1024]) for i in range(2)]
    ytm = xtm
    wsl = [sb(f"wsl{i}", [128, 16, 512], BF16) for i in range(2)]
    sqb = sb("sqb", [128, 2, 512], BF16)
    guT = sb("guT", [128, 8, CM], BF16); attnT = sb("attnT", [128, 8, CM], BF16)
    scr = [sb(f"scr{i}", [128, 512]) for i in range(3)]
    scr.append(scr[1])
    rstd = scr[2]
    sstat = sb("sstat", [128, 64])
    hS = sb("hS", [128, 88, 2]); hlast = sb("hlast", [128, 88, 2]); stT = sb("stT", [128, 88, 4])
    cS = sb("cS", [128, 2, 2]); dummy = sb("fdummy", [128, 2]); hvb = sb("hvbs", [128, 1])
    AW = 17470
    arena = sb("arena", [128, AW])

    def carver():
        off = [0]

        def carve(shape, dt=F32):
            n = int(np.prod(shape[1:]))
            nw_ = n if dt == F32 else (n + 1) // 2
            assert off[0] + nw_ <= AW, (off[0], nw_)
            ap = arena[:, off[0]:off[0] + nw_]
            off[0] += nw_
            if dt != F32:
                ap = ap.bitcast(dt)[:, 0:n]
            if len(shape) == 3:
                ap = ap.rearrange("p (a b) -> p a b", a=shape[1])
            return ap
        return carve
    ca = carver()
    xTh_raw = ca([128, 2048])
    xTh = xTh_raw.rearrange("p (a b) -> p a b", a=16)
    xnh = ca([128, 16, 256], BF16)
    qT = ca([128, 8, CM], BF16); kT = ca([128, 4, CM], BF16); kTh = ca([128, 4, 256], BF16)
    kTs = ca([128, 8, 128], BF16).rearrange("p (s g) k -> p s g k", s=2)
    vdup = ca([128, 5, 512], BF16); vduph = ca([128, 2, 512], BF16); vdups = ca([128, 2, 512], BF16)
    gvn = [ca([128, 1024], BF16) for i in range(3)]
    gvnh = ca([128, 1024], BF16); gvns = ca([128, 1024], BF16)
    gvf = [xTh_raw[:, 0:1024]] * 2
    gvo = xTh_raw[:, 1024:2048]
    kdup2 = [ca([128, 512], BF16) for i in range(2)]; qn2 = [ca([128, 512], BF16) for i in range(2)]
    kdup = kdup2[0]; qn = qn2[0]
    knf = ca([128, 256]); vf = ca([128, 256])
    ebuf = [ca([128, 512], BF16) for i in range(2)]
    pbuf = [ca([128, 512], BF16) for i in range(2)]
    pbuf2 = [pbuf, [ca([128, 512], BF16) for i in range(2)], [ca([128, 512], BF16) for i in range(2)]]
    cb2 = carver()
    mergedT = cb2([128, 16, CM], BF16)
    gsb = [cb2([128, 4, CM], BF16) for i in range(2)]
    actb = [cb2([128, 11, CM], BF16) for i in range(2)]
    hbuf = [cb2([128, 516]) for i in range(2)]
    cbuf = [cb2([128, 512]) for i in range(2)]
    sgb = cb2([128, 516])
    wppb = cb2([128, 2, D], BF16)
    maskab = sb("maskab", [128, 2, NH, 128], BF16)
    masks = sb("masks", [128, 5, NH, 4], BF16)
    ident = sb("ident", [128, 128]); identb = sb("identb", [128, 128], BF16); tri = sb("tri", [128, 128])
    onesn = sb("onesn", [128, 128], BF16); ones1 = sb("ones1", [128, 128], BF16)
    epsb = sb("epsb", [128, 1])
    nw = [sb(f"nw{i}", [128, 16]) for i in range(3)]
    qnw = sb("qnw", [128, 64]); knw = sb("knw", [128, 64]); sinkb = sb("sinkb", [128, NH]); sinkx = sb("sinkx", [128, 4, 2, 2])
    sxh = sb("sxh", [1, 4, 2, 2], BF16); sxl = sb("sxl", [1, 4, 2, 2], BF16); sxt = sb("sxt", [1, 4, 2, 2])
    snw = sb("snw", [128, 1024]); wsT = sb("wsT", [128, 4, 128], BF16)
    bh = sb("bh", [1, 512], BF16); bl = sb("bl", [1, 512], BF16)
    w00 = sb("w00", [2, 4]); w00I = sb("w00I", [2, 4, 2], BF16)
    bgT = sb("bgT", [128, 32]); cwT = sb("cwT", [128, 3, 88]); cbT = sb("cbT", [128, 88])
    ps = [es.enter_context(nc.psum_tensor(f"ps{i}", [128, 512], F32)) for i in range(8)]

    bankc = [0]

    def bank():
        b = bankc[0] % 8
        bankc[0] += 1
        return b

    def PK(b):
        return ("ps", b)

    def mm(out, lhsT, rhs, start, stop, reads, writes):
        S.add("pe", lambda e: e.matmul(out=out, lhsT=lhsT, rhs=rhs, start=start, stop=stop), reads, writes)

    def tr(out, in_, idn, reads, writes):
        S.add("pe", lambda e: e.transpose(out=out, in_=in_, identity=idn), reads, writes)

    def act(out, in_, func, reads, writes, bias=None, scale=None, accum=None):
        kw = {}
        if bias is not None:
            kw["bias"] = bias
        if scale is not None:
            kw["scale"] = scale
        if accum is not None:
            kw["accum_out"] = accum
        S.add("act", lambda e: e.activation(out=out, in_=in_, func=func, **kw), reads, writes)

    def tt(out, in0, in1, op, reads, writes, eng="dve"):
        S.add(eng, lambda e: e.tensor_tensor(out=out, in0=in0, in1=in1, op=op), reads, writes)

    def ts(out, in0, s1, s2, op0, op1, reads, writes, eng="dve"):
        if s2 is None:
            S.add(eng, lambda e: e.tensor_scalar(out=out, in0=in0, scalar1=s1, scalar2=None, op0=op0), reads, writes)
        else:
            S.add(eng, lambda e: e.tensor_scalar(out=out, in0=in0, scalar1=s1, scalar2=s2, op0=op0, op1=op1), reads, writes)

    def stt(out, in0, scalar, in1, op0, op1, reads, writes, eng="dve"):
        S.add(eng, lambda e: e.scalar_tensor_tensor(out=out, in0=in0, scalar=scalar, in1=in1, op0=op0, op1=op1), reads, writes)

    def cp(out, in_, reads, writes, eng="dve"):
        if eng == "act":
            act(out, in_, AF.Copy, reads, writes)
        else:
            S.add(eng, lambda e: e.tensor_copy(out=out, in_=in_), reads, writes)

    def recip(ap, key):
        S.add("dve", lambda e: e.reciprocal(out=ap, in_=ap), [key], [key])

    def dma(eng, out, in_, reads, writes, dsem, slow=False):
        if slow:
            S.add(eng, lambda e: e.dma_start(out=out, in_=in_, allow_slow_non_contiguous=True), reads, writes, dsem=dsem)
        else:
            S.add(eng, lambda e: e.dma_start(out=out, in_=in_), reads, writes, dsem=dsem)

    cdc = [0]

    const_ops = []

    def cdma(out, in_, writes, slow=False, eng="sp"):
        cdc[0] += 1
        dma(eng, out, in_, [], list(writes) + [("cthr", cdc[0] % 4)], f"k{cdc[0]}", slow)

    cdma(ident[:], ident_d[:, :], ["ident"]); cdma(tri[:], tri_d[:, :], ["tri"])
    vslot = [0]

    def vecT(src, n, dst, dkey):
        o = (vslot[0] % 8) * 128
        vslot[0] += 1
        vstage = xtm[1]
        cdma(vstage[0:n, o:o + 128], src.rearrange("(c p) -> c p", p=128), [("xtm", 1)])
        b = bank()
        tr(ps[b][:, 0:n], vstage[0:n, o:o + 128], ident[0:n, 0:n], [("xtm", 1), "ident"], [PK(b)])
        cp(dst, ps[b][:, 0:n], [PK(b)], [dkey])

    for i, d_ in enumerate((nw1_d, nw2_d, nw3_d)):
        vecT(d_, 16, nw[i][:], ("nw", i))
    cdma(qnw[:], qnw_d.partition_broadcast(128), ["qnw"]); cdma(knw[:], knw_d.partition_broadcast(128), ["knw"])
    cdma(sinkb[:], sink_d.partition_broadcast(128), ["sinkb"]); cdma(snw[:], snw_d.partition_broadcast(128), ["snw"])
    bfull = scr[0][0:1, :]
    btmp = scr[1][0:1, :]
    cdma(bfull, sgub_d.partition_broadcast(1), [("scr", 0)])
    cdma(w00[:], sguw_d[:, 0, 0].partition_broadcast(2), ["w00"], slow=True)
    vecT(bg_d, 32, bgT[:], "bgT")
    vecT(cb_d, 88, cbT[:], "cbT")
    for k_ in range(3):
        vecT(cw_d[k_], 88, cwT[:, k_, :], "cwT")
    S.add("pool", lambda e: e.dma_start(out=maskab[:], in_=maskab_d.rearrange("p (r h q) -> p r h q", r=2, h=NH), max_dma_last_dim=2048),
          [], ["maskab"], dsem="cm0")
    dma("pool", masks[:], masks_d.rearrange("p (k h q) -> p k h q", k=5, h=NH), [], ["masks"], "cm1")
    S.add("dve", lambda e: e.memset(onesn[:], 1.0 / 2048.0), [], ["onesn"])
    S.add("dve", lambda e: e.memset(ones1[:], 1.0), [], ["ones1"])
    S.add("dve", lambda e: e.memset(epsb[:], EPS), [], ["epsb"])
    cp(identb[:], ident[:], ["ident"], ["identb"])
    act(sinkb[:], sinkb[:], AF.Exp, ["sinkb"], ["sinkb"])
    cp(sinkx[:], sinkb[:, :].rearrange("p (g h2 par) -> p g par h2", g=4, h2=2, par=2), ["sinkb"], ["sinkx"])
    cp(sxh[:], sinkx[0:1], ["sinkx"], ["sxh"])
    tt(sxt[:], sinkx[0:1], sxh[:], ALU.subtract, ["sinkx", "sxh"], ["sxt"])
    cp(sxl[:], sxt[:], ["sxt"], ["sxl"])
    cp(bh[:], bfull, [("scr", 0)], ["bh"])
    tt(btmp, bfull, bh[:], ALU.subtract, [("scr", 0), "bh"], [("scr", 1)])
    cp(bl[:], btmp, [("scr", 1)], ["bl"])
    for g in range(4):
        ts(w00I[:, g, :], ident[0:2, 0:2], w00[0:2, g:g + 1], None, ALU.mult, None, ["ident", "w00"], [("w00I", g)])
    for g in range(4):
        b = bank()
        cdma(xtm[0][:, g * 128:(g + 1) * 128], sguw_d[g], [("xtm", 0)])
        tr(ps[b][:, 0:128], xtm[0][:, g * 128:(g + 1) * 128], ident[:], [("xtm", 0), "ident"], [PK(b)])
        tt(wsT[:, g, :], ps[b][:, 0:128], tri[:], ALU.mult, [PK(b), "tri"], [("wsT", g)])
    WST = [("wsT", g) for g in range(4)]

    if _os.environ.get("DBGV", "") == "early":
        for _i in range(int(_os.environ.get("NEARLY", "1"))):
            dma("sp", ko[0, :, 0:128], ident[:], ["ident"], [], "oK")
    wcnt = [0]

    def wload(parts, keys_extra=()):
        s = wcnt[0] % 2
        wcnt[0] += 1
        for (fn, src, (cc0, cc1)) in parts:
            wk = [("wsl", s, q_) for q_ in range(cc0 // 128, (cc1 - 1) // 128 + 1)]
            dma("pool", fn(wsl[s]), src, [], wk, f"w{s}")
        return s

    def tm2fm(src, nrows, ncols, evac, slot, koff=0):
        st = xtm[slot]
        dma("sp", st[0:nrows, 0:ncols], src, [], [("xtm", slot)], f"xin{slot}")
        nch = ncols // 128
        per = max(1, min(4, 512 // nrows)) if nrows >= 128 else 16
        for q0 in range(0, nch, per):
            n = min(per, nch - q0)
            b = bank()
            for i in range(n):
                tr(ps[b][:, i * nrows:(i + 1) * nrows], st[0:nrows, (q0 + i) * 128:(q0 + i + 1) * 128], ident[0:nrows, 0:nrows],
                   [("xtm", slot), "ident"], [PK(b)])
            evac(ps[b][:, 0:n * nrows].rearrange("p (a r) -> p a r", a=n), q0 + koff, n, PK(b))

    def norm_fm(xbuf, xkey, cols, nwi, obuf, okey, ocol=None):
        c0, c1 = cols
        n = c1 - c0
        oc0 = c0 if ocol is None else ocol
        b = None
        for kc in range(16):
            act(sqb[:, kc % 2, 0:n], xbuf[:, kc, c0:c1], AF.Square, [xkey(kc)], [("sqb", kc % 2)])
            if kc == 0:
                b = bank()
            mm(ps[b][:, 0:n], onesn[:], sqb[:, kc % 2, 0:n], kc == 0, kc == 15, ["onesn", ("sqb", kc % 2)], [PK(b)])
        act(rstd[:, 0:n], ps[b][:, 0:n], AF.Sqrt, [PK(b), "epsb"], [("scr", 2)], bias=epsb[:], scale=1.0)
        recip(rstd[:, 0:n], ("scr", 2))
        for kc in range(16):
            stt(obuf[:, kc, oc0:oc0 + n], xbuf[:, kc, c0:c1], nw[nwi][:, kc:kc + 1], rstd[:, 0:n], ALU.mult, ALU.mult,
                [xkey(kc), ("nw", nwi), ("scr", 2)], [(okey[0][0], kc)])

    TILES = (("a", 0, 258), ("b", 258, 516))

    def pipe(units, depth=1):
        n = len(units)
        for i in range(min(depth, n)):
            units[i][0]()
        for i in range(n):
            if i + depth < n:
                units[i + depth][0]()
            units[i][1]()

    def K(name, t):
        return [(name, t)]

    pending = []

    def step():
        if pending:
            pending.pop(0)()

    def fm_group(nmc, kparts, evac, accum_keys):
        for mc in range(nmc):
            step()
            for (tn, c0, c1) in TILES:
                b = bank()
                kp = kparts(mc)
                for i, (lh, rf, rd) in enumerate(kp):
                    mm(ps[b][:, 0:c1 - c0], lh, rf(c0, c1), i == 0, i == len(kp) - 1, rd(tn), [PK(b)])
                evac(mc, tn, c0, c1, ps[b][:, 0:c1 - c0], PK(b))

    for j in range(nj):
      try:
        ckpt(0)
        S.add("dve", lambda e: e.memset(dummy[:], 0.0), [], ["ARENA"])
        dma("sp", hvb[:], hvb_d[j], [], ["hvb"], "c1")
        slot = [0]

        def nslot():
            slot[0] += 1
            return slot[0] % 2

        def halo_prep(hb):
            def ev(pa, q0, n, pk):
                cp(xTh[:, q0:q0 + n, :], pa, [pk], [("xTh", c_) for c_ in range(q0, q0 + n)], eng="act")
            for hf in range(2):
                tm2fm(xh[j, hb * 128:(hb + 1) * 128, hf * 1024:(hf + 1) * 1024], 128, 1024, ev, nslot(), koff=8 * hf)
            norm_fm(xTh, (lambda kc: ("xTh", kc)), (0, 128), 0, xnh, [("xnh",)], ocol=hb * 128)
        for blk in range(4):
            def ev(pa, q0, n, pk, blk=blk):
                cp(xT[:, q0:q0 + n, blk * 128:(blk + 1) * 128], pa, [pk], [("xT", c_) for c_ in range(q0, q0 + n)], eng="act")
            for hf in range(2):
                tm2fm(xm[j, blk * 128:(blk + 1) * 128, hf * 1024:(hf + 1) * 1024], 128, 1024, ev, nslot(), koff=8 * hf)
        for hf in range(2):
            tm2fm(xs[j, :, hf * 1024:(hf + 1) * 1024], 4, 1024,
                  lambda pa, q0, n, pk: cp(xT[:, q0:q0 + n, 512:516], pa, [pk], [("xT", c_) for c_ in range(q0, q0 + n)], eng="act"), nslot(), koff=8 * hf)
        def deferred_state():
            for s_ in range(2):
                sl_ = nslot()
                dma("sp", xtm[sl_][:, 0:256], sk[j, s_], [], [("xtm", sl_)], f"xin{sl_}")
                dma("sp", xtm[sl_][:, 256:512], sv[j, s_], [], [("xtm", sl_)], f"xin{sl_}")
                cp(kdup[:].rearrange("p (g t d) -> p g t d", g=4, t=2),
                   xtm[sl_][:, 0:256].rearrange("p (g d) -> p g d", g=4).unsqueeze(2).broadcast_to([128, 4, 2, 64]),
                   [("xtm", sl_)], [("kdup", 0)])
                cp(vdups[:, s_, :].rearrange("p (g t d) -> p g t d", g=4, t=2),
                   xtm[sl_][:, 256:512].rearrange("p (g d) -> p g d", g=4).unsqueeze(2).broadcast_to([128, 4, 2, 64]),
                   [("xtm", sl_)], [("vdups",)], eng="act")
                b = bank()
                pb = ps[b][:].bitcast(BF16)
                for g in range(4):
                    tr(pb[:, g * 128:(g + 1) * 128], kdup[:, g * 128:(g + 1) * 128], identb[:], [("kdup", 0), "identb"], [PK(b)])
                cp(kTs[:, s_, :, :], pb[:, 0:512].rearrange("p (g k) -> p g k", g=4), [PK(b)], [("kTs",)], eng="act")


        def deferred_p_sc():
            for blk in range(4):
                def ev(pa, q0, n, pk, blk=blk):
                    cp(pT[:, q0:q0 + n, blk * 128:(blk + 1) * 128], pa, [pk], [("pT",)], eng="dve")
                pending.append(lambda ev=ev, blk=blk: tm2fm(pm[j, blk * 128:(blk + 1) * 128, :], 128, 256, ev, nslot()))
            pending.append(lambda: tm2fm(pm[j, 512:516, :], 4, 256, lambda pa, q0, n, pk: cp(pT[:, q0:q0 + n, 512:516], pa, [pk], [("pT",)], eng="dve"), nslot()))
            for pc in range(0, 88, 8):
                n_ = min(8, 88 - pc)
                def ev(pa, q0, n, pk, pc=pc):
                    cp(stT[:, pc + q0:pc + q0 + n, :], pa, [pk], [("stT",)], eng="dve")
                pending.append(lambda ev=ev, pc=pc, n_=n_: tm2fm(sc[j, :, pc * 128:(pc + n_) * 128], 4, n_ * 128, ev, nslot()))

        XM = lambda kc: ("xT", kc)
        for (_tn, _c0, _c1) in TILES:
            norm_fm(xT, XM, (_c0, _c1), 0, xnT, [("xn",)])

        ckpt(1)
        def win(c0):
            return wload([((lambda w, h_=h_: w[:, :, h_ * 256:(h_ + 1) * 256]),
                           w_in[:, c0 + h_ * 256:c0 + (h_ + 1) * 256].rearrange("(kc p) n -> p kc n", p=128),
                           (h_ * 256, (h_ + 1) * 256)) for h_ in range(2)])

        MB = [("m%d" % i, xnT, i * 128, 128, [("xn",)]) for i in range(4)]
        SBk = ("s", xnT, 512, 4, [("xn",)])
        HB = [("h%d" % i, xnh, i * 128, 128, [("xnh",)]) for i in range(2)]

        def tm_mm(blk, s, b):
            _, xb, c0, np_, xk = blk
            for kc in range(16):
                mm(ps[b][0:np_, :], xb[:, kc, c0:c0 + np_], wsl[s][:, kc, :], kc == 0, kc == 15, [(xk[0][0], kc), ("wsl", s)], [PK(b)])

        def headnorm(src, np_, nh_, wt, wkey, dst, dkey, srckey, sidx):
            sc_ = scr[sidx]
            act(sc_[0:np_, 0:nh_ * 64], src, AF.Square, srckey, [("scr", sidx)])
            S.add("dve", lambda e: e.tensor_reduce(out=sstat[0:np_, 0:nh_], in_=sc_[0:np_, 0:nh_ * 64].rearrange("p (h d) -> p h d", h=nh_),
                                                     axis=AX.X, op=ALU.add), [("scr", sidx)], ["sstat"])
            act(sstat[0:np_, 0:nh_], sstat[0:np_, 0:nh_], AF.Sqrt, ["sstat", "epsb"], ["sstat"], bias=epsb[0:np_, :], scale=1.0 / 64.0)
            recip(sstat[0:np_, 0:nh_], "sstat")
            tt(sc_[0:np_, 0:nh_ * 64].rearrange("p (h d) -> p h d", h=nh_), src.rearrange("p (h d) -> p h d", h=nh_),
               sstat[0:np_, 0:nh_].unsqueeze(2).broadcast_to([np_, nh_, 64]), ALU.mult, srckey + ["sstat"], [("scr", sidx)])
            tt(dst.rearrange("p (h d) -> p h d", h=nh_), sc_[0:np_, 0:nh_ * 64].rearrange("p (h d) -> p h d", h=nh_),
               wt[0:np_, :].unsqueeze(1).broadcast_to([np_, nh_, 64]), ALU.mult, [("scr", sidx), wkey], dkey)

        s = win(1024)
        kv_units = []
        for bi, blk in enumerate(MB + [SBk] + HB):
            st_ = {}

            def pre(bi=bi, blk=blk, st_=st_, s=s):
                name, xb, c0, np_, xk = blk
                kd_ = kdup2[bi % 2]
                KD = ("kdup", bi % 2)
                b = bank()
                tm_mm(blk, s, b)
                headnorm(ps[b][0:np_, 0:256], np_, 4, knw, "knw", knf[0:np_, :], ["knf"], [PK(b)], 0)
                cp(kd_[0:np_, :].rearrange("p (g t d) -> p g t d", g=4, t=2),
                   knf[0:np_, :].rearrange("p (g d) -> p g d", g=4).unsqueeze(2).broadcast_to([np_, 4, 2, 64]), ["knf"], [KD])
                if name.startswith("h"):
                    vd, vkey = vduph[:, int(name[1]), :], ("vduph",)
                elif name == "s":
                    vd, vkey = vdup[:, 4, :], ("vdup", 4)
                else:
                    vd, vkey = vdup[:, int(name[1]), :], ("vdup", int(name[1]))
                cp(vd[0:np_, :].rearrange("p (g t d) -> p g t d", g=4, t=2),
                   ps[b][0:np_, 256:512].rearrange("p (g d) -> p g d", g=4).unsqueeze(2).broadcast_to([np_, 4, 2, 64]),
                   [PK(b)], [vkey], eng="act")
                if name == "m3" or name == "s":
                    cp(vf[0:np_, :], ps[b][0:np_, 256:512], [PK(b)], ["vf"], eng="dve")
                    if name == "m3":
                        dma("sp", ko[j], knf[:], ["knf"], [], "oK")
                        dma("sp", vo[j], vf[:], ["vf"], [], "oV")
                    else:
                        dma("sp", kso[j, :, 127, :], knf[0:2, :], ["knf"], [], "oK"); dma("sp", vso[j, :, 127, :], vf[0:2, :], ["vf"], [], "oV")

            def post(bi=bi, blk=blk):
                name, xb, c0, np_, xk = blk
                kd_ = kdup2[bi % 2]
                KD = ("kdup", bi % 2)
                b2 = bank()
                pb = ps[b2][:].bitcast(BF16)
                for g in range(4):
                    tr(pb[:, g * np_:(g + 1) * np_], kd_[0:np_, g * 128:(g + 1) * 128], identb[0:np_, 0:np_], [KD, "identb"], [PK(b2)])
                src_ = pb[:, 0:4 * np_].rearrange("p (g k) -> p g k", g=4)
                if name.startswith("h"):
                    cp(kTh[:, :, c0:c0 + np_], src_, [PK(b2)], [("kTh",)], eng="act")
                else:
                    cp(kT[:, :, c0:c0 + np_], src_, [PK(b2)], [("kT", name)], eng="act")
            kv_units.append((pre, post))
        hp0 = ((lambda: halo_prep(0)), (lambda: None))
        hp1 = ((lambda: halo_prep(1)), (lambda: None))
        kv_units = kv_units[0:2] + [hp0] + kv_units[2:3] + [hp1] + kv_units[3:]

        ckpt(2)
        q_units = []
        qcnt = [0]
        for half in range(2):
            qs = {}
            for bi, blk in enumerate(MB + [SBk]):
                qi = qcnt[0] % 2
                qcnt[0] += 1

                def pre(bi=bi, blk=blk, qs=qs, half=half, qi=qi):
                    if bi == 0:
                        qs["s"] = win(half * 512)
                    name, xb, c0, np_, xk = blk
                    b = bank()
                    tm_mm(blk, qs["s"], b)
                    headnorm(ps[b][0:np_, :], np_, 8, qnw, "qnw", qn2[qi][0:np_, :], [("qn", qi)], [PK(b)], 1)

                def post(bi=bi, blk=blk, half=half, qi=qi):
                    name, xb, c0, np_, xk = blk
                    qn_ = qn2[qi]
                    b2 = bank()
                    pb = ps[b2][:].bitcast(BF16)
                    for c in range(4):
                        tr(pb[:, c * np_:(c + 1) * np_], qn_[0:np_, c * 128:(c + 1) * 128], identb[0:np_, 0:np_], [("qn", qi), "identb"], [PK(b2)])
                    cp(qT[:, half * 4:half * 4 + 4, c0:c0 + np_], pb[:, 0:4 * np_].rearrange("p (g k) -> p g k", g=4), [PK(b2)],
                       [("qT", name, half)], eng="act")
                q_units.append((pre, post))
        pipe(kv_units + q_units)
        deferred_state()

        ckpt(3)
        deferred_p_sc()
        for half in range(2):
            s = win(1536 + half * 512)

            def kparts(mc, s=s):
                return [(wsl[s][:, kc, mc * 128:(mc + 1) * 128], (lambda c0, c1, kc=kc: xnT[:, kc, c0:c1]),
                         (lambda tn, mc=mc, kc=kc: [("xn", kc), ("wsl", s, mc)])) for kc in range(16)]

            def evac(mc, tn, c0, c1, pa, pk, half=half):
                act(guT[:, half * 4 + mc, c0:c1], pa, AF.Gelu_apprx_tanh, [pk], [("guT", half * 4 + mc)])
            fm_group(4, kparts, evac, None)

        ckpt(4)
        s0 = win(2560); s1 = win(3072)
        S.add("dve", lambda e: e.memset(dummy[:], 0.0), [], [("xTh", c_) for c_ in range(16)] + [("gvf", 0), "gvo"])
        gv_units = []
        for ui, blk in enumerate([HB[1]] + MB + [SBk]):
            name = blk[0]
            if name == "h1":
                gb, gk = gvnh, ("gvnh",)
            elif name == "s":
                gb, gk = gvns, ("gvns",)
            else:
                gb, gk = gvn[ui % 3], ("gvn", ui % 3)

            def pre(blk=blk, gb=gb, gk=gk):
                name, xb, c0, np_, xk = blk
                b0 = bank(); b1 = bank()
                tm_mm(blk, s0, b0); tm_mm(blk, s1, b1)
                gf = gvf[0]
                FK = ("gvf", 0)
                act(gf[0:np_, 0:512], ps[b0][0:np_, :], AF.Gelu_apprx_tanh, [PK(b0)], [FK])
                act(gf[0:np_, 512:1024], ps[b1][0:np_, :], AF.Gelu_apprx_tanh, [PK(b1)], [FK])
                S.add("dve", lambda e, gf=gf, np_=np_: e.tensor_reduce(out=sstat[0:np_, 32:33], in_=gf[0:np_, :], axis=AX.X, op=ALU.add), [FK], ["sstat2"])
                ts(sstat[0:np_, 33:34], sstat[0:np_, 32:33], -1.0 / 1024.0, None, ALU.mult, None, ["sstat2"], ["sstat3"])
                act(gvo[0:np_, :], gf[0:np_, :], AF.Square, [FK, "sstat3"], ["gvo", "sstat4"], bias=sstat[0:np_, 33:34], scale=1.0,
                    accum=sstat[0:np_, 34:35])
                act(sstat[0:np_, 35:36], sstat[0:np_, 34:35], AF.Sqrt, ["sstat4", "epsb"], ["sstat5"], bias=epsb[0:np_, :], scale=1.0 / 1024.0)
                recip(sstat[0:np_, 35:36], "sstat5")
                tt(sstat[0:np_, 36:37], sstat[0:np_, 33:34], sstat[0:np_, 35:36], ALU.mult, ["sstat3", "sstat5"], ["sstat6"])
                act(gf[0:np_, :], gf[0:np_, :], AF.Identity, [FK, "sstat5", "sstat6"], [FK], bias=sstat[0:np_, 36:37], scale=sstat[0:np_, 35:36])
                tt(gb[0:np_, :], gf[0:np_, :], snw[0:np_, :], ALU.mult, [FK, "snw"], [gk])
                if name == "m3" or name == "s":
                    tt(gvo[0:np_, :], gf[0:np_, :], snw[0:np_, :], ALU.mult, [FK, "snw"], ["gvo"])
                    if name == "m3":
                        dma("sp", go[j], gvo[:], ["gvo"], [], "oG")
                    else:
                        dma("sp", gso[j], gvo[0:2, :], ["gvo"], [], "oG")

            def post(blk=blk, gb=gb, gk=gk):
                name = blk[0]
                if not name.startswith("m"):
                    return
                bi = int(name[1])
                for hh in range(2):
                    b = bank()
                    for c4 in range(4):
                        cc = hh * 4 + c4
                        g = cc // 2
                        o = ps[b][:, c4 * 128:(c4 + 1) * 128]
                        mm(o, gb[:, cc * 128:(cc + 1) * 128], wsT[:, g, :], True, False, [gk] + WST, [PK(b)])
                        mm(o, ones1[0:1, :], bh[0:1, g * 128:(g + 1) * 128], False, False, ["ones1", "bh"], [PK(b)])
                        mm(o, ones1[0:1, :], bl[0:1, g * 128:(g + 1) * 128], False, True, ["ones1", "bl"], [PK(b)])
                    gk4 = [("guT", hh * 4 + c) for c in range(4)]
                    tt(guT[:, hh * 4:hh * 4 + 4, bi * 128:(bi + 1) * 128], ps[b][:].rearrange("p (c t) -> p c t", c=4),
                       guT[:, hh * 4:hh * 4 + 4, bi * 128:(bi + 1) * 128], ALU.mult, [PK(b)] + gk4, gk4)
            gv_units.append((pre, post))
        pipe(gv_units, depth=2)
        b = bank()
        for cc in range(8):
            g = cc // 2
            o = ps[b][:, cc * 4:cc * 4 + 2]
            mm(o, gvns[0:2, cc * 128:(cc + 1) * 128], w00I[0:2, g, :], True, False, [("gvns",), ("w00I", g)], [PK(b)])
            mm(o, ones1[0:1, :], bh[0:1, g * 128:g * 128 + 1].broadcast_to([1, 2]), False, False, ["ones1", "bh"], [PK(b)])
            mm(o, ones1[0:1, :], bl[0:1, g * 128:g * 128 + 1].broadcast_to([1, 2]), False, True, ["ones1", "bl"], [PK(b)])
            o = ps[b][:, cc * 4 + 2:cc * 4 + 4]
            mm(o, gvnh[:, cc * 128:(cc + 1) * 128], wsT[:, g, 126:128], True, False, [("gvnh",)] + WST, [PK(b)])
            mm(o, ones1[0:1, :], bh[0:1, g * 128 + 126:g * 128 + 128], False, False, ["ones1", "bh"], [PK(b)])
            mm(o, ones1[0:1, :], bl[0:1, g * 128 + 126:g * 128 + 128], False, True, ["ones1", "bl"], [PK(b)])
        gk8 = [("guT", c) for c in range(8)]
        tt(guT[:, :, 512:516], ps[b][:, 0:32].rearrange("p (c t) -> p c t", c=8), guT[:, :, 512:516], ALU.mult, [PK(b)] + gk8, gk8)

        ckpt(5)
        def attn_norm(bn, bd, g, nq, qc0, tn):
            rc = scr[2]
            S.add("dve", lambda e: e.reciprocal(out=rc[:, 0:4 * nq], in_=ps[bd][:, 0:4 * nq]), [PK(bd)], [("scr", 2)])
            for par in range(2):
                lo, hi = par * 64, par * 64 + 64
                tt(attnT[lo:hi, 2 * g:2 * g + 2, qc0:qc0 + nq], ps[bn][lo:hi, par * 2 * nq:(par + 1) * 2 * nq].rearrange("p (b q) -> p b q", b=2),
                   rc[lo:hi, par * 2 * nq:(par + 1) * 2 * nq].rearrange("p (b q) -> p b q", b=2), ALU.mult, [PK(bn), ("scr", 2)],
                   [("attnT", g, par)])

        at_units = []
        for c in range(4):
            qc0 = c * 128
            if c == 0:
                kA, vA, mA, rdA = kTh[:, :, 128:256], vduph[:, 1, :], maskab, [("kTh",), ("vduph",), "maskab"]
            else:
                kA, vA, mA, rdA = kT[:, :, qc0 - 128:qc0], vdup[:, c - 1, :], maskab, [("kT", "m%d" % (c - 1)), ("vdup", c - 1), "maskab"]
            kB, vB = kT[:, :, qc0:qc0 + 128], vdup[:, c, :]
            rdB = [("kT", "m%d" % c), ("vdup", c)]
            for g in range(4):
                ui = len(at_units)

                def pre(c=c, g=g, qc0=qc0, kA=kA, kB=kB, mA=mA, rdA=rdA, rdB=rdB, ui=ui):
                    pbs = pbuf2[ui % 3]
                    bP = [bank(), bank()]
                    for par in range(2):
                        lo, hi = par * 64, par * 64 + 64
                        for r, kk in enumerate((kA, kB)):
                            mm(ps[bP[par]][:, r * 256:(r + 1) * 256], kk[lo:hi, g, :], qT[lo:hi, 2 * g:2 * g + 2, qc0:qc0 + 128], True, True,
                               rdA + rdB + [("qT", "m%d" % c, g // 2)], [PK(bP[par])])
                        if c == 0:
                            act(ebuf[par][:, 0:256], ps[bP[par]][:, 0:256], AF.Exp, [PK(bP[par]), "hvb"], [("ebuf", par)], scale=0.125, bias=hvb[:])
                            act(ebuf[par][:, 256:512], ps[bP[par]][:, 256:512], AF.Exp, [PK(bP[par])], [("ebuf", par)], scale=0.125)
                        else:
                            act(ebuf[par][:], ps[bP[par]][:], AF.Exp, [PK(bP[par])], [("ebuf", par)], scale=0.125)
                        tt(pbs[par][:].rearrange("p (r b q) -> p r b q", r=2, b=2), ebuf[par][:].rearrange("p (r b q) -> p r b q", r=2, b=2),
                           mA[:, :, 4 * g + par:4 * g + par + 3:2, :], ALU.mult, [("ebuf", par)] + rdA, [("pbuf", ui % 3, par)])

                def post(c=c, g=g, qc0=qc0, vA=vA, vB=vB, rdA=rdA, rdB=rdB, ui=ui):
                    pbs = pbuf2[ui % 3]
                    bn, bd = bank(), bank()
                    for (bo, use_v) in ((bn, True), (bd, False)):
                        for par in range(2):
                            for r, vv in enumerate((vA, vB)):
                                lh = vv[:, g * 128:(g + 1) * 128] if use_v else ones1[:]
                                mm(ps[bo][:, par * 256:(par + 1) * 256], lh, pbs[par][:, r * 256:(r + 1) * 256], r == 0, (r == 1) and use_v,
                                   rdA + rdB + [("pbuf", ui % 3, par), "ones1"], [PK(bo)])
                            if not use_v:
                                for si_, (sx_, sk_) in enumerate(((sxh, "sxh"), (sxl, "sxl"))):
                                    mm(ps[bo][:, par * 256:(par + 1) * 256], ones1[0:1, :],
                                       sx_[0:1, g, par, :].unsqueeze(2).broadcast_to([1, 2, 128]), False, si_ == 1, ["ones1", sk_], [PK(bo)])
                    attn_norm(bn, bd, g, 128, qc0, "m")
                at_units.append((pre, post))
        pipe(at_units, depth=2)
        ckpt(6)
        kbs = [(kTs[:, 0, :, :], vdups[:, 0, :], 128, [("kTs",), ("vdups",)]),
               (kTs[:, 1, :, :], vdups[:, 1, :], 128, [("kTs",), ("vdups",)]),
               (kT[:, :, 512:514], vdup[:, 4, :], 2, [("kT", "s"), ("vdup", 4)]),
               (kTh[:, :, 0:128], vduph[:, 0, :], 128, [("kTh",), ("vduph",)]),
               (kTh[:, :, 128:256], vduph[:, 1, :], 128, [("kTh",), ("vduph",)])]
        for g in range(4):
            bPs = [bank(), bank()]
            pPs = [ps[bb_][:, 0:40].rearrange("p (k b q) -> p k b q", k=5, b=2) for bb_ in bPs]
            for par in range(2):
                lo, hi = par * 64, par * 64 + 64
                for ki, (kk, vv, nk, rd) in enumerate(kbs):
                    mm(pPs[par][0:nk, ki, :, :], kk[lo:hi, g, :], qT[lo:hi, 2 * g:2 * g + 2, 512:516], True, True,
                       rd + [("qT", "s", g // 2)], [PK(bPs[par])])
            eb = ebuf[0][:, 0:80].rearrange("p (a k b q) -> p a k b q", a=2, k=5, b=2)
            pb_ = pbuf[0][:, 0:80].rearrange("p (a k b q) -> p a k b q", a=2, k=5, b=2)
            for par in range(2):
                for (k0, k1, np2) in ((0, 2, 128), (3, 5, 128), (2, 3, 2)):
                    act(eb[0:np2, par, k0:k1], pPs[par][0:np2, k0:k1], AF.Exp, [PK(bPs[par])], [("ebuf", 0)], scale=0.125)
                    tt(pb_[0:np2, par, k0:k1], eb[0:np2, par, k0:k1], masks[0:np2, k0:k1, 4 * g + par:4 * g + par + 3:2, :], ALU.mult,
                       [("ebuf", 0), "masks"], [("pbuf", 0, 0)])
            bn, bd = bank(), bank()
            for (bo, use_v) in ((bn, True), (bd, False)):
                for par in range(2):
                    for ki, (kk, vv, nk, rd) in enumerate(kbs):
                        lh = vv[0:nk, g * 128:(g + 1) * 128] if use_v else ones1[0:nk, :]
                        mm(ps[bo][:, par * 8:(par + 1) * 8], lh, pb_[0:nk, par, ki, :, :], ki == 0, (ki == 4) and use_v, rd + [("pbuf", 0, 0), "ones1"], [PK(bo)])
                    if not use_v:
                        for si_, (sx_, sk_) in enumerate(((sxh, "sxh"), (sxl, "sxl"))):
                            mm(ps[bo][:, par * 8:(par + 1) * 8], ones1[0:1, :],
                               sx_[0:1, g, par, :].unsqueeze(2).broadcast_to([1, 2, 4]), False, si_ == 1, ["ones1", sk_], [PK(bo)])
            attn_norm(bn, bd, g, 4, 512, "s")

        ckpt(7)
        S.add("dve", lambda e: e.memset(dummy[:], 0.0), [], ["ARENA"])
        AK = lambda tn: [("attnT", g_, p_) for g_ in range(4) for p_ in range(2)]
        GK = lambda tn: [("guT", c_) for c_ in range(8)]
        for cg in range(4):
            for which in range(2):
                gc0 = which * D + cg * 512
                s = wload([((lambda w, h_=h_: w[:, :, h_ * 256:(h_ + 1) * 256]),
                            w_gate[:, gc0 + h_ * 256:gc0 + (h_ + 1) * 256].rearrange("(kc p) n -> p kc n", p=128),
                            (h_ * 256, (h_ + 1) * 256)) for h_ in range(2)])

                def kparts(mc, s=s):
                    return [(wsl[s][:, kc, mc * 128:(mc + 1) * 128], (lambda c0, c1, kc=kc: xnT[:, kc, c0:c1]),
                             (lambda tn, mc=mc, kc=kc: [("xn", kc), ("wsl", s, mc)])) for kc in range(16)]

                def evac(mc, tn, c0, c1, pa, pk, which=which, cg=cg):
                    ch = which * 16 + cg * 4 + mc
                    act(gsb[which][:, mc, c0:c1], pa, AF.Sigmoid, [pk, "bgT"], [("gsb", which, mc)], bias=bgT[:, ch:ch + 1], scale=1.0)
                fm_group(4, kparts, evac, None)
            brp = []
            for h_ in range(2):
                brp.append(((lambda w, h_=h_: w[:, 0:8, h_ * 256:(h_ + 1) * 256]),
                            w_bra[:, cg * 512 + h_ * 256:cg * 512 + (h_ + 1) * 256].rearrange("(kc p) n -> p kc n", p=128), (h_ * 256, (h_ + 1) * 256)))
                brp.append(((lambda w, h_=h_: w[:, 8:16, h_ * 256:(h_ + 1) * 256]),
                            w_brg[:, cg * 512 + h_ * 256:cg * 512 + (h_ + 1) * 256].rearrange("(kc p) n -> p kc n", p=128), (h_ * 256, (h_ + 1) * 256)))
            s = wload(brp)
            for mc in range(4):
                for (tn, c0, c1) in TILES:
                    ba, bm = bank(), bank()
                    n = c1 - c0
                    for kc in range(8):
                        mm(ps[ba][:, 0:n], wsl[s][:, kc, mc * 128:(mc + 1) * 128], attnT[:, kc, c0:c1], kc == 0, kc == 7, AK(tn) + [("wsl", s, mc)], [PK(ba)])
                    for kc in range(8):
                        mm(ps[bm][:, 0:n], wsl[s][:, 8 + kc, mc * 128:(mc + 1) * 128], guT[:, kc, c0:c1], kc == 0, kc == 7, GK(tn) + [("wsl", s, mc)], [PK(bm)])
                    tt(scr[3][:, 0:n], ps[ba][:, 0:n], gsb[1][:, mc, c0:c1], ALU.mult, [PK(ba), ("gsb", 1, mc)], [("scr", 1)])
                    tt(scr[0][:, 0:n], ps[bm][:, 0:n], gsb[0][:, mc, c0:c1], ALU.mult, [PK(bm), ("gsb", 0, mc)], [("scr", 0)])
                    tt(mergedT[:, cg * 4 + mc, c0:c1], scr[3][:, 0:n], scr[0][:, 0:n], ALU.add, [("scr", 1), ("scr", 0)], [("mg", cg * 4 + mc)])
        MK = lambda tn: [("mg", c_) for c_ in range(16)]

        def resid_evac(cgv):
            def evac(mc, tn, c0, c1, pa, pk):
                ch = cgv * 4 + mc
                tt(xT[:, ch, c0:c1], pa, xT[:, ch, c0:c1], ALU.add, [pk, ("xT", ch)], [("xT", ch)])
            return evac

        for cg in range(4):
            s = wload([((lambda w, h_=h_: w[:, :, h_ * 256:(h_ + 1) * 256]),
                        w_out[:, cg * 512 + h_ * 256:cg * 512 + (h_ + 1) * 256].rearrange("(kc p) n -> p kc n", p=128),
                        (h_ * 256, (h_ + 1) * 256)) for h_ in range(2)])

            def kparts(mc, s=s):
                return [(wsl[s][:, kc, mc * 128:(mc + 1) * 128], (lambda c0, c1, kc=kc: mergedT[:, kc, c0:c1]),
                         (lambda tn, mc=mc: MK(tn) + [("wsl", s, mc)])) for kc in range(16)]
            fm_group(4, kparts, resid_evac(cg), None)

        ckpt(8)
        while pending:
            step()
        for s_ in range(2):
            for (src_t, dst_t) in ((sk, kso), (sv, vso)):
                dma("sp", dst_t[j, s_, 0:112, :], src_t[j, s_, 1:113, :], [], [], "o1")
                dma("sp", dst_t[j, s_, 112:127, :], src_t[j, s_, 113:128, :], [], [], "o1")
            dma("sp", cso[j, s_, 0:1, :], sc[j, 2 * s_ + 1:2 * s_ + 2, :], [], [], "o1")
        dma("pool", wppb[:], w_pp.rearrange("(kc p) n -> p kc n", p=128), [], ["wppb"], "cm2")
        for (_tn, _c0, _c1) in TILES:
            norm_fm(xT, XM, (_c0, _c1), 1, xnT, [("xn",)])

        def up_group(G):
            ab = actb[G % 2]
            AKEY = ("actb", G % 2)
            jl = list(range(G * 11, G * 11 + 11))
            for f0 in range(0, 11, 2):
                pairs = jl[f0:f0 + 2]
                npair = len(pairs)
                j0 = pairs[0]
                s = wload([(lambda w, npair=npair: w[:, :, 0:npair * 128], w_up[:, j0 * 128:(j0 + npair) * 128].rearrange("(kc p) n -> p kc n", p=128), (0, npair * 128)),
                           (lambda w, npair=npair: w[:, :, 256:256 + npair * 128], w_up[:, DFF + j0 * 128:DFF + (j0 + npair) * 128].rearrange("(kc p) n -> p kc n", p=128), (256, 256 + npair * 128))])
                for pi, jj in enumerate(pairs):
                    jloc = jj - G * 11
                    bks = []
                    for part in range(2):
                        wc0 = part * 256 + pi * 128
                        bm_, bs_ = bank(), bank()
                        bks.append((bm_, bs_))
                        for (bb, c0, c1, tn) in ((bm_, 0, 258, "a"), (bs_, 258, 516, "b")):
                            for kc in range(16):
                                mm(ps[bb][:, 0:c1 - c0], wsl[s][:, kc, wc0:wc0 + 128], xnT[:, kc, c0:c1], kc == 0, kc == 15,
                                   [("xn", kc), ("wsl", s, part * 2 + pi)], [PK(bb)])
                    for part in range(2):
                        ch = part * 44 + jj
                        bm_, bs_ = bks[part]
                        hb_, cb_ = hbuf[part], cbuf[part]
                        sc2, bi2 = cwT[:, 2, ch:ch + 1], cbT[:, ch:ch + 1]
                        act(hb_[:, 2:260], ps[bm_][:, 0:258], AF.Copy, [PK(bm_)], [("hbuf", part, 0)])
                        act(cb_[:, 0:258], ps[bm_][:, 0:258], AF.Identity, [PK(bm_), "cwT", "cbT"], [("cbuf", part, 0)], bias=bi2, scale=sc2)
                        act(hb_[:, 260:514], ps[bs_][:, 0:254], AF.Copy, [PK(bs_)], [("hbuf", part, 1)])
                        act(cb_[:, 258:512], ps[bs_][:, 0:254], AF.Identity, [PK(bs_), "cwT", "cbT"], [("cbuf", part, 1)], bias=bi2, scale=sc2)
                        act(hb_[:, 0:2], ps[bs_][:, 256:258], AF.Copy, [PK(bs_)], [("hbuf", part, 2)])
                        act(hS[:, ch, :], ps[bs_][:, 254:256], AF.Copy, [PK(bs_)], [("hS", ch)])
                        act(cS[:, part, :], ps[bs_][:, 254:256], AF.Identity, [PK(bs_), "cwT", "cbT"], [("cS", part)], bias=bi2, scale=sc2)
                    HK = lambda p_: [("hbuf", p_, 0), ("hbuf", p_, 1), ("hbuf", p_, 2)]
                    CK = lambda p_: [("cbuf", p_, 0), ("cbuf", p_, 1)]
                    for tap, (o0, o1) in ((1, (1, 513)), (0, (0, 512))):
                        for part in range(2):
                            ch = part * 44 + jj
                            stt(cbuf[part][:], hbuf[part][:, o0:o1], cwT[:, tap, ch:ch + 1], cbuf[part][:], ALU.mult, ALU.add,
                                HK(part) + ["cwT"] + CK(part), CK(part))
                    for part in range(2):
                        ch = part * 44 + jj
                        cp(hlast[:, ch, :], hbuf[part][:, 512:514], HK(part), [("hlast", ch)])
                    for tap, sl2 in ((1, slice(1, 4, 2)), (0, slice(0, 3, 2))):
                        for part in range(2):
                            ch = part * 44 + jj
                            stt(cS[:, part, :], stT[:, ch, sl2], cwT[:, tap, ch:ch + 1], cS[:, part, :], ALU.mult, ALU.add,
                                [("stT",), "cwT", ("cS", part)], [("cS", part)])
                    act(sgb[:, 0:512], cbuf[0][:], AF.Silu, CK(0), ["sgb"])
                    act(sgb[:, 512:514], cS[:, 0, :], AF.Silu, [("cS", 0)], ["sgb2"])
                    tt(ab[:, jloc, 0:512], sgb[:, 0:512], cbuf[1][:], ALU.mult, ["sgb"] + CK(1), [AKEY])
                    tt(ab[:, jloc, 512:514], sgb[:, 512:514], cS[:, 1, :], ALU.mult, ["sgb2", ("cS", 1)], [(AKEY[0], AKEY[1], "s")])
                    S.add("dve", lambda e, ab=ab, jloc=jloc: e.memset(ab[:, jloc, 514:516], 0.0), [], [(AKEY[0], AKEY[1], "z")])

        def down_group(G):
            ab = actb[G % 2]
            AKEY = ("actb", G % 2)
            for cg in range(4):
                s = wload([((lambda w, h_=h_: w[:, 0:11, h_ * 256:(h_ + 1) * 256]),
                            w_down[G * 11 * 128:(G + 1) * 11 * 128, cg * 512 + h_ * 256:cg * 512 + (h_ + 1) * 256].rearrange("(kc p) n -> p kc n", p=128),
                            (h_ * 256, (h_ + 1) * 256)) for h_ in range(2)])

                def kparts(mc, s=s):
                    return [(wsl[s][:, jl_, mc * 128:(mc + 1) * 128], (lambda c0, c1, jl_=jl_: ab[:, jl_, c0:c1]),
                             (lambda tn: [AKEY, (AKEY[0], AKEY[1], "s"), (AKEY[0], AKEY[1], "z"), ("wsl", s, mc)])) for jl_ in range(11)]
                fm_group(4, kparts, resid_evac(cg), None)

        up_group(0); up_group(1); down_group(0); up_group(2); down_group(1); up_group(3); down_group(2); down_group(3)

        def fm2tm_small(srcbuf, skeys_fn, dst_fn, dsem):
            for pc in range(0, 88, 8):
                n_ = min(8, 88 - pc)
                sl_ = nslot()
                for q0 in range(0, n_, 4):
                    b = bank()
                    for i in range(4):
                        tr(ps[b][0:2, i * 128:(i + 1) * 128], srcbuf[:, pc + q0 + i, :], ident[:], [skeys_fn(pc + q0 + i), "ident"], [PK(b)])
                    cp(ytm[sl_][0:2, q0 * 128:(q0 + 4) * 128], ps[b][0:2, :], [PK(b)], [("xtm", sl_)])
                dma("sp", dst_fn(pc * 128, (pc + n_) * 128), ytm[sl_][0:2, 0:n_ * 128], [("xtm", sl_)], [], f"xin{sl_}")
        fm2tm_small(hlast, lambda ch: ("hlast", ch), lambda a, b_: co[j, :, a:b_], "o3")
        fm2tm_small(hS, lambda ch: ("hS", ch), lambda a, b_: cso[j, :, 1, a:b_], "o3")

        ckpt(9)
        for (_tn, _c0, _c1) in TILES:
            norm_fm(xT, XM, (_c0, _c1), 2, xnT, [("xn",)])
        for cg in range(4):
            s = wload([((lambda w, h_=h_: w[:, :, h_ * 256:(h_ + 1) * 256]),
                        w_pg[:, cg * 512 + h_ * 256:cg * 512 + (h_ + 1) * 256].rearrange("(kc p) n -> p kc n", p=128),
                        (h_ * 256, (h_ + 1) * 256)) for h_ in range(2)])
            for mc in range(4):
                ch = cg * 4 + mc
                for (tn, c0, c1) in TILES:
                    n = c1 - c0
                    bg_, bp_ = bank(), bank()
                    for kc in range(16):
                        mm(ps[bg_][:, 0:n], wsl[s][:, kc, mc * 128:(mc + 1) * 128], xnT[:, kc, c0:c1], kc == 0, kc == 15, [("xn", kc), ("wsl", s, mc)], [PK(bg_)])
                    for kc in range(2):
                        mm(ps[bp_][:, 0:n], wppb[:, kc, ch * 128:(ch + 1) * 128], pT[:, kc, c0:c1], kc == 0, kc == 1, [("pT",), "wppb"], [PK(bp_)])
                    act(scr[3][:, 0:n], ps[bg_][:, 0:n], AF.Sigmoid, [PK(bg_)], [("scr", 1)])
                    tt(scr[3][:, 0:n], scr[3][:, 0:n], ps[bp_][:, 0:n], ALU.mult, [("scr", 1), PK(bp_)], [("scr", 1)])
                    tt(xT[:, ch, c0:c1], scr[3][:, 0:n], xT[:, ch, c0:c1], ALU.add, [("scr", 1), ("xT", ch)], [("xT", ch)])

        ckpt(10)
        for blk in range(4):
            for hf in range(2):
                sl_ = nslot()
                for q2 in range(2):
                    q = hf * 2 + q2
                    b = bank()
                    for i in range(4):
                        tr(ps[b][:, i * 128:(i + 1) * 128], xT[:, q * 4 + i, blk * 128:(blk + 1) * 128], ident[:], [("xT", q * 4 + i), "ident"], [PK(b)])
                    cp(ytm[sl_][:, q2 * 512:(q2 + 1) * 512], ps[b][:], [PK(b)], [("xtm", sl_)], eng=("act" if q % 2 else "dve"))
                dma("sp", ym[j, blk * 128:(blk + 1) * 128, hf * 1024:(hf + 1) * 1024], ytm[sl_][:], [("xtm", sl_)], [], f"xin{sl_}")
        for hf in range(2):
            sl_ = nslot()
            for q2 in range(2):
                q = hf * 2 + q2
                b = bank()
                for i in range(4):
                    tr(ps[b][0:2, i * 128:(i + 1) * 128], xT[:, q * 4 + i, 512:514], ident[:], [("xT", q * 4 + i), "ident"], [PK(b)])
                cp(ytm[sl_][0:2, q2 * 512:(q2 + 1) * 512], ps[b][0:2, :], [PK(b)], [("xtm", sl_)])
            dma("sp", ys[j, :, hf * 1024:(hf + 1) * 1024], ytm[sl_][0:2, :], [("xtm", sl_)], [], f"xin{sl_}")

      except _Stop:
        pass
    S.emit(nc)
    es.close()
    return nc, S


def _consts():
    h = np.arange(1, NH + 1, dtype=np.float64)
    sl = np.exp2(-8.0 * h / NH)
    jj = np.arange(128)[:, None, None].astype(np.float64)
    ii = np.arange(128)[None, None, :].astype(np.float64)
    s3 = sl[None, :, None]
    mc = np.where(ii >= jj, np.exp(-s3 * (ii - jj)), 0.0)
    mp = np.where(ii <= jj, np.exp(-s3 * (128.0 + ii - jj)), 0.0)
    maskab = np.stack([mp, mc], axis=1).astype(np.float32)
    mask0 = maskab.copy()
    mask0[:, 0] = 0.0
    ms = np.zeros((128, 5, NH, 4), np.float32)
    ms[:, 0, :, 0] = mp[:, :, 0]
    ms[:, 1, :, 1] = mp[:, :, 0]
    ms[0, 2, :, 0] = 1.0
    ms[1, 2, :, 1] = 1.0
    ms[:, 3, :, 2] = mp[:, :, 126]; ms[:, 3, :, 3] = mp[:, :, 127]
    ms[:, 4, :, 2] = mc[:, :, 126]; ms[:, 4, :, 3] = mc[:, :, 127]
    tri = (np.arange(128)[:, None] <= np.arange(128)[None, :]).astype(np.float32)
    return dict(maskab=maskab.reshape(128, -1), mask0=mask0.reshape(128, -1), masks=ms.reshape(128, -1),
                ident=np.eye(128, dtype=np.float32), tri=tri)


_PROG = {}


def make_core_inputs(inp, jobs, C):
    f = np.float32
    nj = len(jobs)
    xm = np.zeros((nj, 512, D), f); xh = np.zeros((nj, 256, D), f); xs = np.zeros((nj, 4, D), f)
    pm = np.zeros((nj, 516, 256), f); sk = np.zeros((nj, 2, 128, 256), f); sv = np.zeros((nj, 2, 128, 256), f)
    sc = np.zeros((nj, 4, 2 * DFF), f); hvb = np.zeros((nj, 128, 1), f)
    for i, (b, qd, ss) in enumerate(jobs):
        t0 = qd * 512
        xm[i] = inp["x_prompt"][b, t0:t0 + 512]
        pm[i, 0:512] = inp["p_prompt"][0, b, t0:t0 + 512]
        if qd > 0:
            xh[i] = inp["x_prompt"][b, t0 - 256:t0]
            xs[i, 2:4] = inp["x_prompt"][b, t0 - 2:t0]
        else:
            hvb[i] = -30000.0
        for k, s_ in enumerate(ss):
            xs[i, k] = inp["x_sample"][s_, 0]
            pm[i, 512 + k] = inp["p_sample"][0, s_, 0]
            sk[i, k] = inp["state_attn_k"][0, s_].reshape(128, 256)
            sv[i, k] = inp["state_attn_v"][0, s_].reshape(128, 256)
            sc[i, 2 * k:2 * k + 2] = inp["state_conv"][0, s_]
    m = dict(xm=xm, xh=xh, xs=xs, pm=pm, sk=sk, sv=sv, sc=sc, hvb=hvb,
             maskab=C["maskab"], masks=C["masks"], ident=C["ident"], tri=C["tri"])
    for k in ("w_in", "w_br_attn", "w_br_gm", "w_gate", "w_out", "w_up", "w_down", "w_ple_gate", "w_ple_proj", "sgu_w"):
        m[k] = np.ascontiguousarray(inp[k][0], dtype=f)
    for k in ("attn_norm_w", "ffn_norm_w", "ple_norm_w", "q_norm_w", "k_norm_w", "attn_sinks", "sgu_norm_w", "b_gate", "conv_b"):
        m[k] = np.ascontiguousarray(inp[k][0], dtype=f)
    m["sgu_b"] = np.ascontiguousarray(inp["sgu_b"][0].reshape(512), dtype=f)
    m["conv_w"] = np.ascontiguousarray(inp["conv_w"][0], dtype=f)
    return m


def kernel(**inputs):
    inp = {k: np.asarray(v) for k, v in inputs.items()}
    if "nc" not in _PROG:
        _PROG["nc"] = build_program(NJ)[0]
    nc = _PROG["nc"]
    C = _consts()
    core_jobs = []
    for c in range(8):
        jobs = []
        for jl in range(NJ):
            jg = c * NJ + jl
            jobs.append((jg // 4, jg % 4, [2 * jg, 2 * jg + 1]))
        core_jobs.append(jobs)
    in_maps = [make_core_inputs(inp, jobs, C) for jobs in core_jobs]
    res = run_bass_kernel_spmd(nc, in_maps, core_ids=list(range(8)))
    f = np.float32
    y_p = np.zeros((4, 2048, D), f); y_s = np.zeros((32, 1, D), f)
    akp = np.zeros((1, 4, 128, 4, 64), f); avp = np.zeros((1, 4, 128, 4, 64), f)
    aks = np.zeros((1, 32, 128, 4, 64), f); avs = np.zeros((1, 32, 128, 4, 64), f)
    gp = np.zeros((1, 4, 128, 1024), f); gs = np.zeros((1, 32, 1, 1024), f)
    cpo = np.zeros((1, 4, 2, 2 * DFF), f); cs = np.zeros((1, 32, 2, 2 * DFF), f)
    for c in range(8):
        r = res.results[c]
        for i, (b, qd, ss) in enumerate(core_jobs[c]):
            y_p[b, qd * 512:(qd + 1) * 512] = r["ym"][i]
            for k, s_ in enumerate(ss):
                y_s[s_, 0] = r["ys"][i, k]
                aks[0, s_] = r["kso"][i, k].reshape(128, 4, 64); avs[0, s_] = r["vso"][i, k].reshape(128, 4, 64)
                gs[0, s_, 0] = r["gso"][i, k]
                cs[0, s_] = r["cso"][i, k]
            if qd == 3:
                akp[0, b] = r["ko"][i].reshape(128, 4, 64); avp[0, b] = r["vo"][i].reshape(128, 4, 64)
                gp[0, b] = r["go"][i]
                cpo[0, b] = r["co"][i]
    return (y_p, y_s, akp, avp, aks, avs, gp, gs, cpo, cs)
```
